# Optimizing a Trainium2 kernel written in Bass

```python
import math
import jax, jax.numpy as jnp
from jax import lax
import numpy as np

D_MODEL = 1024
BATCH = 32
SEQ = 2048
DEPTH = 2

HEAD_DIM = 64
N_HEADS = D_MODEL // HEAD_DIM
N_KV_HEADS = N_HEADS // 4
QKV_WIDTH = (N_HEADS + 2 * N_KV_HEADS) * HEAD_DIM
ATTN_HALF_WINDOW = 128
DILATED_GROUPS = ((128, 1), (512, 4), (2048, 16))
N_DGROUPS = len(DILATED_GROUPS)
N_MIXERS = 2
N_LAYERS_A = (DEPTH + 1) // 2
N_LAYERS_B = DEPTH // 2
D_FF = -(-8 * D_MODEL // (3 * 256)) * 256
ROPE_THETA = 10000.0
RMS_EPS = 1e-6
NEG_INF = -1e30

kernel_name = "hybrid_window_sink_dilated_encoder"


def rmsnorm(x, g):
    x32 = x.astype(jnp.float32)
    y = x32 * lax.rsqrt(jnp.mean(x32 * x32, axis=-1, keepdims=True) + RMS_EPS)
    return (y * g.astype(jnp.float32)).astype(x.dtype)


def rope_tables(seq):
    inv_freq = 1.0 / (ROPE_THETA ** (jnp.arange(0, HEAD_DIM, 2, dtype=jnp.float32) / HEAD_DIM))
    ang = jnp.arange(seq, dtype=jnp.float32)[:, None] * inv_freq[None, :]
    return jnp.cos(ang)[:, None, :], jnp.sin(ang)[:, None, :]


def apply_rope(t, cos, sin):
    t32 = t.astype(jnp.float32)
    t1, t2 = jnp.split(t32, 2, axis=-1)
    out = jnp.concatenate([t1 * cos - t2 * sin, t2 * cos + t1 * sin], axis=-1)
    return out.astype(t.dtype)


def split_qkv(proj, cos, sin):
    b, s, _ = proj.shape
    qw = N_HEADS * HEAD_DIM
    kw = N_KV_HEADS * HEAD_DIM
    q = proj[..., :qw].reshape(b, s, N_HEADS, HEAD_DIM)
    k = proj[..., qw:qw + kw].reshape(b, s, N_KV_HEADS, HEAD_DIM)
    v = proj[..., qw + kw:].reshape(b, s, N_KV_HEADS, HEAD_DIM)
    return apply_rope(q, cos, sin), apply_rope(k, cos, sin), v


def banded_attention(q, k, v, half_window, sink=None):
    n, length, n_q, dh = q.shape
    n_kv = k.shape[2]
    grp = n_q // n_kv
    w = half_window
    nb = -(-length // w)
    lp = nb * w
    qb = jnp.pad(q, ((0, 0), (0, lp - length), (0, 0), (0, 0))).reshape(n, nb, w, n_kv, grp, dh)
    pad_kv = ((0, 0), (w, w + lp - length), (0, 0), (0, 0))
    kp = jnp.pad(k, pad_kv)
    vp = jnp.pad(v, pad_kv)
    scale = 1.0 / math.sqrt(dh)
    offs_q = jnp.arange(w)
    offs_k = jnp.arange(3 * w) - w
    sink_l = None if sink is None else sink.astype(jnp.float32).reshape(n_kv, grp)[None, :, :, None]

    def one_block(i):
        start = i * w
        q_i = lax.dynamic_index_in_dim(qb, i, axis=1, keepdims=False)
        k_i = lax.dynamic_slice_in_dim(kp, start, 3 * w, axis=1)
        v_i = lax.dynamic_slice_in_dim(vp, start, 3 * w, axis=1)
        s = jnp.einsum('nqkgd,nskd->nkgqs', q_i, k_i).astype(jnp.float32) * scale
        qpos = start + offs_q
        kpos = start + offs_k
        valid = ((jnp.abs(qpos[:, None] - kpos[None, :]) <= w)
                 & (kpos[None, :] >= 0) & (kpos[None, :] < length))
        s = jnp.where(valid, s, NEG_INF)
        m = jnp.max(s, axis=-1)
        if sink_l is not None:
            m = jnp.maximum(m, sink_l)
        p = jnp.exp(s - m[..., None])
        denom = jnp.sum(p, axis=-1)
        if sink_l is not None:
            denom = denom + jnp.exp(sink_l - m)
        o = jnp.einsum('nkgqs,nskd->nqkgd', p, v_i.astype(jnp.float32))
        o = o / jnp.transpose(denom, (0, 3, 1, 2))[..., None]
        lse = jnp.transpose(m + jnp.log(denom), (0, 3, 1, 2))
        return o.astype(q.dtype), lse

    o, lse = lax.map(one_block, jnp.arange(nb))
    o = jnp.moveaxis(o, 0, 1).reshape(n, lp, n_q, dh)[:, :length]
    lse = jnp.moveaxis(lse, 0, 1).reshape(n, lp, n_q)[:, :length]
    return o, lse


def dilated_attention(q, k, v, dilation, half_window):
    b, s, n_q, dh = q.shape
    d = dilation

    def to_residue(t):
        return t.reshape(b, s // d, d, t.shape[2], dh).transpose(0, 2, 1, 3, 4).reshape(b * d, s // d, t.shape[2], dh)

    o, lse = banded_attention(to_residue(q), to_residue(k), to_residue(v), half_window // d)
    o = o.reshape(b, d, s // d, n_q, dh).transpose(0, 2, 1, 3, 4).reshape(b, s, n_q, dh)
    lse = lse.reshape(b, d, s // d, n_q).transpose(0, 2, 1, 3).reshape(b, s, n_q)
    return o, lse


def mixer_window_sink(h, w_in, sink, w_out, cos, sin):
    b, s, _ = h.shape
    q, k, v = split_qkv(h @ w_in, cos, sin)
    o, _ = banded_attention(q, k, v, ATTN_HALF_WINDOW, sink)
    return o.reshape(b, s, N_HEADS * HEAD_DIM) @ w_out


def mixer_dilated(h, w_in, w_out, cos, sin):
    b, s, _ = h.shape
    proj = (h @ w_in).reshape(b, s, N_DGROUPS, QKV_WIDTH)
    outs, lses = [], []
    for g, (window, dilation) in enumerate(DILATED_GROUPS):
        q, k, v = split_qkv(proj[:, :, g], cos, sin)
        o, lse = dilated_attention(q, k, v, dilation, window // 2)
        outs.append(o)
        lses.append(lse)
    wts = jax.nn.softmax(jnp.stack(lses, axis=0), axis=0)
    o = (wts[0][..., None] * outs[0].astype(jnp.float32)
         + wts[1][..., None] * outs[1].astype(jnp.float32)
         + wts[2][..., None] * outs[2].astype(jnp.float32))
    return o.astype(h.dtype).reshape(b, s, N_HEADS * HEAD_DIM) @ w_out


def swiglu(h, w_gate, w_up, w_down):
    return (jax.nn.silu(h @ w_gate) * (h @ w_up)) @ w_down


def setup_inputs(seed: int = 0) -> dict:
    key = jax.random.key(seed)
    ks = jax.random.split(key, 14)
    f32 = jnp.float32
    d = D_MODEL
    hd = N_HEADS * HEAD_DIM
    x = jax.random.normal(ks[0], (BATCH, SEQ, d), f32)
    a_w_in = jax.random.normal(ks[1], (N_LAYERS_A, d, QKV_WIDTH), f32) * d ** -0.5
    a_sink = jax.random.normal(ks[2], (N_LAYERS_A, N_HEADS), f32) * 0.5
    a_w_out = jax.random.normal(ks[3], (N_LAYERS_A, hd, d), f32) * hd ** -0.5
    b_w_in = jax.random.normal(ks[4], (N_LAYERS_B, d, N_DGROUPS * QKV_WIDTH), f32) * d ** -0.5
    b_w_out = jax.random.normal(ks[5], (N_LAYERS_B, hd, d), f32) * hd ** -0.5
    norm_mix = 1.0 + 0.02 * jax.random.normal(ks[6], (DEPTH, d), f32)
    norm_ffn = 1.0 + 0.02 * jax.random.normal(ks[7], (DEPTH, d), f32)
    w_gate = jax.random.normal(ks[8], (DEPTH, d, D_FF), f32) * d ** -0.5
    w_up = jax.random.normal(ks[9], (DEPTH, d, D_FF), f32) * d ** -0.5
    w_down = jax.random.normal(ks[10], (DEPTH, D_FF, d), f32) * D_FF ** -0.5
    final_norm = 1.0 + 0.02 * jax.random.normal(ks[11], (d,), f32)
    return {"x": x, "a_w_in": a_w_in, "a_sink": a_sink, "a_w_out": a_w_out,
            "b_w_in": b_w_in, "b_w_out": b_w_out, "norm_mix": norm_mix, "norm_ffn": norm_ffn,
            "w_gate": w_gate, "w_up": w_up, "w_down": w_down, "final_norm": final_norm}


def reference(x, a_w_in, a_sink, a_w_out, b_w_in, b_w_out, norm_mix, norm_ffn,
              w_gate, w_up, w_down, final_norm):
    cos, sin = rope_tables(x.shape[1])
    for i in range(DEPTH):
        h = rmsnorm(x, norm_mix[i])
        j = i // N_MIXERS
        if i % N_MIXERS == 0:
            mix = mixer_window_sink(h, a_w_in[j], a_sink[j], a_w_out[j], cos, sin)
        else:
            mix = mixer_dilated(h, b_w_in[j], b_w_out[j], cos, sin)
        x = x + mix
        h = rmsnorm(x, norm_ffn[i])
        x = x + swiglu(h, w_gate[i], w_up[i], w_down[i])
    return rmsnorm(x, final_norm)
```

```python
import contextlib
import numpy as np
import concourse.bass as bass
import concourse.mybir as mybir
from concourse.bass_utils import run_bass_kernel_spmd

F32 = mybir.dt.float32
BF16 = mybir.dt.bfloat16
AF = mybir.ActivationFunctionType
ALU = mybir.AluOpType

S = 2048
D = 1024
DFF = 2816
NCH = 8
NFC = 22
TB = 512
NTB = 4
NCORES = 8
SEQ_PER_CORE = 4
ENGS = ("pe", "act", "dve", "pool", "sp")
SAME_ENGINE_SYNC = True
MASK_ON_PE = (True, False)


class Prog:
    def __init__(self, nc):
        self.nc = nc
        self.ops = []
        self.iv = {}
        self.dma_slots = {}
        self.ps_free = list(range(8))

    def _access(self, idx, eng_id, key, is_write, deps):
        space, lo, hi = key
        lst = self.iv.setdefault(space, [])
        new = []
        for rec in lst:
            l, h, i, w, e = rec
            if l < hi and lo < h:
                if is_write or w:
                    deps.add(i)
                if is_write and lo <= l and h <= hi:
                    continue
                if (not is_write) and (not w) and e == eng_id and l == lo and h == hi:
                    continue
            new.append(rec)
        new.append((lo, hi, idx, is_write, eng_id))
        self.iv[space] = new

    def _record(self, eng, emit, reads, writes, dma):
        idx = len(self.ops)
        deps = set()
        eng_id = eng if dma is None else ("dma", idx)
        for k in reads:
            self._access(idx, eng_id, k, False, deps)
        for k in writes:
            self._access(idx, eng_id, k, True, deps)
        deps.discard(idx)
        self.ops.append(dict(eng=eng, emit=emit, deps=deps, dma=dma, signal=False))
        return idx

    def op(self, eng, emit, reads=(), writes=()):
        return self._record(eng, emit, reads, writes, None)

    def dma(self, queue, emit, slot, reads=(), writes=()):
        cnt = self.dma_slots.get(slot, 0) + 16
        self.dma_slots[slot] = cnt
        return self._record(queue, emit, reads, writes, (slot, cnt))

    def ps_alloc(self):
        assert self.ps_free, "out of PSUM banks"
        return self.ps_free.pop(0)

    def ps_release(self, b):
        self.ps_free.append(b)

    def build(self, final_wait_ops=()):
        nc = self.nc
        ops = self.ops
        def needs_edge(x, o):
            if x["dma"] is not None:
                return True
            if o["dma"] is not None:
                return True
            if x["eng"] != o["eng"]:
                return True
            return SAME_ENGINE_SYNC and x["eng"] != "pe"

        for o in ops:
            for d in o["deps"]:
                x = ops[d]
                if x["dma"] is None and needs_edge(x, o):
                    x["signal"] = True
        counts = {e: 0 for e in ENGS}
        for o in ops:
            if o["dma"] is None and o["signal"]:
                counts[o["eng"]] += 1
                o["cnt"] = counts[o["eng"]]
        with contextlib.ExitStack() as st:
            esem = {e: st.enter_context(nc.semaphore("s_" + e)) for e in ENGS}
            dsem = {s: st.enter_context(nc.semaphore("d_%s" % (s,))) for s in self.dma_slots}
            block = st.enter_context(nc.Block())
            per_eng = {e: [] for e in ENGS}
            for i, o in enumerate(ops):
                per_eng[o["eng"]].append(i)

            def run(eng_name, eng):
                waited = {}
                for i in per_eng[eng_name]:
                    o = ops[i]
                    need = {}
                    for d in o["deps"]:
                        x = ops[d]
                        if x["dma"] is not None:
                            key = ("d", x["dma"][0])
                            val = x["dma"][1]
                        else:
                            if not needs_edge(x, o):
                                continue
                            key = ("e", x["eng"])
                            val = x["cnt"]
                        if val > need.get(key, 0):
                            need[key] = val
                    for key, val in need.items():
                        if waited.get(key, 0) >= val:
                            continue
                        waited[key] = val
                        sem = dsem[key[1]] if key[0] == "d" else esem[key[1]]
                        eng.wait_ge(sem, val)
                    ins = o["emit"](eng)
                    if o["dma"] is not None:
                        ins.then_inc(dsem[o["dma"][0]], 16)
                    elif o["signal"]:
                        ins.then_inc(esem[eng_name], 1)
                if eng_name == "sp":
                    for i in final_wait_ops:
                        x = ops[i]
                        eng.wait_ge(dsem[x["dma"][0]], x["dma"][1])

            @block.tensor
            def _(e):
                run("pe", e)

            @block.scalar
            def _(e):
                run("act", e)

            @block.vector
            def _(e):
                run("dve", e)

            @block.gpsimd
            def _(e):
                run("pool", e)

            @block.sync
            def _(e):
                run("sp", e)
        return counts


class Buf:
    def __init__(self, space, ap2d, byte_off, dt):
        self.space = space
        self.ap = ap2d
        self.off = byte_off
        self.esz = 2 if dt == BF16 else 4

    def k(self, lo=0, hi=None):
        if hi is None:
            hi = self.ap.shape[1]
        return (self.space, self.off + lo * self.esz, self.off + hi * self.esz)


class Arena:
    def __init__(self, t, nbytes):
        self.t = t
        self.nbytes = nbytes
        self.cur = 0
        self.marks = []

    def alloc(self, nelem, dt):
        esz = 2 if dt == BF16 else 4
        nb = (nelem * esz + 63) // 64 * 64
        off = self.cur
        assert off + nb <= self.nbytes, ("arena overflow", off + nb, self.nbytes)
        self.cur += nb
        self.hw = max(getattr(self, 'hw', 0), self.cur)
        v = self.t[:, off // 4:(off + nb) // 4]
        if dt == BF16:
            v = v.bitcast(BF16)
        v = v[:, 0:nelem]
        return Buf("arena", v, off, dt)

    def push(self):
        self.marks.append(self.cur)

    def pop(self):
        self.cur = self.marks.pop()


LAYER_CFG = [
    dict(groups=[(128, 1)], sink=True),
    dict(groups=[(64, 1), (64, 4), (64, 16)], sink=False),
]


def build_program(nseq=SEQ_PER_CORE, nlayers=2, final_norm=True, do_attn=True, do_ffn=True, layer_list=None, dbg_groups=None):
    nc = bass.Bass("TRN2", target_bir_lowering=False)
    dt_in = lambda name, shape: nc.dram_tensor(name, shape, F32, kind="ExternalInput").ap()
    x_d = dt_in("x", [nseq * S, D])
    win_d = [dt_in("a_w_in", [D, 1536]), dt_in("b_w_in", [D, 4608])]
    wout_d = [dt_in("a_w_out", [D, D]), dt_in("b_w_out", [D, D])]
    wg_d = dt_in("w_gate", [2, D, DFF])
    wu_d = dt_in("w_up", [2, D, DFF])
    wd_d = dt_in("w_down", [2, DFF, D])
    gains_d = dt_in("gains", [128, 40])
    sinkb_d = dt_in("sinkb", [128, 8])
    ropec_d = dt_in("ropec", [128, S])
    ropes_d = dt_in("ropes", [128, S])
    ident_d = dt_in("ident", [128, 128])
    rperm_d = dt_in("rperm", [128, 128])
    m128_d = dt_in("m128", [128, 384])
    m64_d = dt_in("m64", [128, 256])
    mm128_d = dt_in("mm128", [128, 384])
    mm64_d = dt_in("mm64", [128, 256])
    out_d = nc.dram_tensor("out", [nseq * S, D], F32, kind="ExternalOutput").ap()

    st = contextlib.ExitStack()
    with st:
        def sbt(name, shape, dt):
            return st.enter_context(nc.sbuf_tensor("sb_" + name, shape, dt))

        xT_t = sbt("xT", [128, NCH * S], F32)
        hT_t = sbt("hT", [128, NCH * S], BF16)
        ropec_t = sbt("ropec", [128, S], F32)
        ropes_t = sbt("ropes", [128, S], F32)
        ident_t = sbt("identf", [128, 128], F32)
        identb_t = sbt("identb", [128, 128], BF16)
        rperm_t = sbt("rperm", [128, 128], BF16)
        ones_t = sbt("ones", [128, 128], BF16)
        m128_t = sbt("m128", [128, 384], BF16)
        m64_t = sbt("m64", [128, 256], BF16)
        mm128_t = sbt("mm128", [128, 384], BF16)
        mm64_t = sbt("mm64", [128, 256], BF16)
        gains_t = sbt("gains", [128, 40], F32)
        es_t = sbt("es", [128, 8], F32)
        ARENA_BYTES = 93952
        arena_t = sbt("arena", [128, ARENA_BYTES // 4], F32)
        ps_t = st.enter_context(nc.psum_tensor("ps", [128, 8, 512], F32))

        P = Prog(nc)
        A = Arena(arena_t, ARENA_BYTES)
        xT = Buf("xT", xT_t[:, :], 0, F32)
        hT = Buf("hT", hT_t[:, :], 0, BF16)
        CONST = ("const", 0, 1)
        CONSTS = None
        KC = {n: ("c_" + n, 0, 1) for n in ("ropec", "ropes", "ident", "gains", "es", "rperm", "m128", "m64", "ones", "identb", "mm")}

        def psk(b):
            return ("ps", b, b + 1)

        def xs(c, t0, n):
            return xT_t[:, c * S + t0:c * S + t0 + n], xT.k(c * S + t0, c * S + t0 + n)

        def hs(c, t0, n, step=1):
            end = t0 + (n - 1) * step + 1
            return hT_t[:, c * S + t0:c * S + end:step], hT.k(c * S + t0, c * S + end)

        P.dma("sp", lambda e: e.dma_start(out=ropec_t[:, :], in_=ropec_d), "c0", writes=[KC["ropec"]])
        P.dma("sp", lambda e: e.dma_start(out=ropes_t[:, :], in_=ropes_d), "c1", writes=[KC["ropes"]])
        P.dma("sp", lambda e: e.dma_start(out=ident_t[:, :], in_=ident_d), "c2", writes=[KC["ident"]])
        P.dma("sp", lambda e: e.dma_start(out=gains_t[:, :], in_=gains_d), "c3", writes=[KC["gains"]])
        P.dma("sp", lambda e: e.dma_start(out=es_t[:, :], in_=sinkb_d), "c4", writes=[KC["es"]])
        P.dma("pool", lambda e: e.dma_start(out=rperm_t[:, :], in_=rperm_d), "c5", writes=[KC["rperm"]])
        P.dma("pool", lambda e: e.dma_start(out=m128_t[:, :], in_=m128_d), "c6", writes=[KC["m128"]])
        P.dma("pool", lambda e: e.dma_start(out=m64_t[:, :], in_=m64_d), "c7", writes=[KC["m64"]])
        P.dma("pool", lambda e: e.dma_start(out=identb_t[:, :], in_=ident_d), "c8", writes=[KC["identb"]])
        P.dma("pool", lambda e: e.dma_start(out=mm128_t[:, :], in_=mm128_d), "c9", writes=[KC["mm"]])
        P.dma("pool", lambda e: e.dma_start(out=mm64_t[:, :], in_=mm64_d), "c10", writes=[KC["mm"]])
        P.op("dve", lambda e: e.memset(ones_t[:, :], 1.0), writes=[KC["ones"]])
        P.op("act", lambda e: e.activation(out=es_t[:, :], in_=es_t[:, :], func=AF.Exp), reads=[KC["es"]], writes=[KC["es"]])

        wslot = [0]
        def scratch(name, shape):
            return nc.dram_tensor("sc_" + name, shape, BF16).ap()
        win_b = [scratch("a_w_in", [D, 1536]), scratch("b_w_in", [D, 4608])]
        wout_b = [scratch("a_w_out", [D, D]), scratch("b_w_out", [D, D])]
        wg_b = scratch("w_gate", [2, D, DFF])
        wu_b = scratch("w_up", [2, D, DFF])
        wd_b = scratch("w_down", [2, DFF, D])
        SK = {}

        def cast1(name, dst, src):
            P.dma("pool", lambda e: e.dma_start(out=dst, in_=src), "cast_" + name, writes=[SK[name]])

        cast_plan = {
            "start": [("win0", win_b[0], win_d[0]), ("wout0", wout_b[0], wout_d[0])],
            "after_load": [("wg0", wg_b[0], wg_d[0]), ("wu0", wu_b[0], wu_d[0]), ("wd0", wd_b[0], wd_d[0])],
            "attn0": [("win1", win_b[1], win_d[1]), ("wout1", wout_b[1], wout_d[1])],
            "ffn0": [("wg1", wg_b[1], wg_d[1]), ("wu1", wu_b[1], wu_d[1]), ("wd1", wd_b[1], wd_d[1])],
        }
        for lst in cast_plan.values():
            for (name, _d, _s) in lst:
                SK[name] = ("sc_" + name, 0, 1)

        def do_casts(tag):
            for (name, dst, src) in cast_plan.pop(tag, []):
                cast1(name, dst, src)

        do_casts("start")

        def wload(dst_ap, dst_key, src_ap, src_key):
            wslot[0] += 1
            P.dma("sp", lambda e: e.dma_start(out=dst_ap, in_=src_ap), "w%d" % (wslot[0] % 40), reads=[src_key], writes=[dst_key])

        def norm_stats(rstd, tbs, sq):
            for tb in tbs:
                b = P.ps_alloc()
                for c in range(NCH):
                    xa, xk = xs(c, tb * TB, TB)
                    sb_ = sq[(tb * NCH + c) % len(sq)]
                    P.op("act", lambda e, sb_=sb_, xa=xa: e.activation(out=sb_.ap, in_=xa, func=AF.Square),
                         reads=[xk], writes=[sb_.k()])
                    P.op("pe", lambda e, sb_=sb_, b=b, c=c: e.matmul(ps_t[:, b, :], lhsT=ones_t[:, :], rhs=sb_.ap,
                                                                      start=(c == 0), stop=(c == NCH - 1)),
                         reads=[sb_.k(), KC["ones"]], writes=[psk(b)])
                ra = rstd.ap[:, tb * TB:(tb + 1) * TB]
                rk = rstd.k(tb * TB, (tb + 1) * TB)
                P.op("act", lambda e, ra=ra, b=b: e.activation(out=ra, in_=ps_t[:, b, :], func=AF.Sqrt, bias=1e-6, scale=1.0 / D),
                     reads=[psk(b)], writes=[rk])
                P.ps_release(b)
                P.op("dve", lambda e, ra=ra: e.reciprocal(out=ra, in_=ra), reads=[rk], writes=[rk])

        def rmsnorm_to_hT(gain_idx, tbs=None):
            tbs = list(range(NTB)) if tbs is None else list(tbs)
            A.push()
            rstd = A.alloc(S, F32)
            sq = [A.alloc(TB, BF16) for _ in range(3)]
            ntmp = [A.alloc(TB, F32) for _ in range(2)]
            norm_stats(rstd, tbs, sq)
            for tb in tbs:
                ra = rstd.ap[:, tb * TB:(tb + 1) * TB]
                rk = rstd.k(tb * TB, (tb + 1) * TB)
                for c in range(NCH):
                    xa, xk = xs(c, tb * TB, TB)
                    ha, hk = hs(c, tb * TB, TB)
                    if c % 2 == 0:
                        P.op("dve", lambda e, ha=ha, xa=xa, ra=ra, c=c: e.scalar_tensor_tensor(
                            out=ha, in0=xa, scalar=gains_t[:, gain_idx * 8 + c:gain_idx * 8 + c + 1], in1=ra,
                            op0=ALU.mult, op1=ALU.mult), reads=[xk, rk, KC["gains"]], writes=[hk])
                    else:
                        tm = ntmp[(c // 2) % 2]
                        P.op("pool", lambda e, tm=tm, xa=xa, ra=ra: e.tensor_tensor(out=tm.ap, in0=xa, in1=ra, op=ALU.mult), reads=[xk, rk], writes=[tm.k()])
                        P.op("act", lambda e, tm=tm, ha=ha, c=c: e.activation(out=ha, in_=tm.ap, func=AF.Copy, scale=gains_t[:, gain_idx * 8 + c:gain_idx * 8 + c + 1]),
                             reads=[tm.k(), KC["gains"]], writes=[hk])
            A.pop()

        def load_x(seq):
            A.push()
            A.alloc(12288, F32)
            xin = [A.alloc(D, F32) for _ in range(4)]
            for tt in range(S // 128):
                xb = xin[tt % 4]
                r0 = seq * S + tt * 128
                P.dma("sp", lambda e, xb=xb, r0=r0: e.dma_start(out=xb.ap, in_=x_d[r0:r0 + 128, :]), "xin%d" % (tt % 4), writes=[xb.k()])
                for half in range(2):
                    b = P.ps_alloc()
                    for j in range(4):
                        c = half * 4 + j
                        P.op("pe", lambda e, xb=xb, b=b, j=j, c=c: e.transpose(out=ps_t[:, b, j * 128:(j + 1) * 128],
                                                                                 in_=xb.ap[:, c * 128:(c + 1) * 128], identity=ident_t[:, :]),
                             reads=[xb.k(), KC["ident"]], writes=[psk(b)])
                    dst = xT_t[:, :].rearrange("p (c t) -> p c t", c=NCH)[:, half * 4:half * 4 + 4, tt * 128:(tt + 1) * 128]
                    src = ps_t[:, b, :].rearrange("p (c t) -> p c t", c=4)
                    keys = [xT.k(c * S + tt * 128, c * S + tt * 128 + 128) for c in range(half * 4, half * 4 + 4)]
                    eng = "dve" if half == 0 else "act"
                    if eng == "dve":
                        P.op("dve", lambda e, dst=dst, src=src: e.tensor_copy(out=dst, in_=src), reads=[psk(b)], writes=keys)
                    else:
                        P.op("act", lambda e, dst=dst, src=src: e.activation(out=dst, in_=src, func=AF.Copy), reads=[psk(b)], writes=keys)
                    P.ps_release(b)
            A.pop()

        def store_out(seq, gain_idx, rstd=None):
            A.push()
            ost = [A.alloc(D, F32) for _ in range(3)]
            ytmp = [A.alloc(128 * NCH, F32) for _ in range(3)]
            last = []
            if gain_idx is not None and rstd is None:
                rstd = A.alloc(S, F32)
                sq = [A.alloc(TB, BF16) for _ in range(3)]
                norm_stats(rstd, range(NTB), sq)
            for tt in range(S // 128):
                yt = ytmp[tt % 3]
                ob = ost[tt % 3]
                for c in range(NCH):
                    xa, xk = xs(c, tt * 128, 128)
                    ya = yt.ap[:, c * 128:(c + 1) * 128]
                    yk = yt.k(c * 128, (c + 1) * 128)
                    eng = "dve" if (c % 2 == 0 or gain_idx is not None) else "pool"
                    if gain_idx is not None:
                        ra = rstd.ap[:, tt * 128:(tt + 1) * 128]
                        rk = rstd.k(tt * 128, (tt + 1) * 128)
                        P.op(eng, lambda e, ya=ya, xa=xa, ra=ra, c=c: e.scalar_tensor_tensor(
                            out=ya, in0=xa, scalar=gains_t[:, gain_idx * 8 + c:gain_idx * 8 + c + 1], in1=ra, op0=ALU.mult, op1=ALU.mult),
                            reads=[xk, rk, KC["gains"]], writes=[yk])
                    else:
                        P.op(eng, lambda e, ya=ya, xa=xa: e.tensor_copy(out=ya, in_=xa), reads=[xk], writes=[yk])
                for half in range(2):
                    b = P.ps_alloc()
                    for j in range(4):
                        c = half * 4 + j
                        P.op("pe", lambda e, yt=yt, b=b, j=j, c=c: e.transpose(out=ps_t[:, b, j * 128:(j + 1) * 128], in_=yt.ap[:, c * 128:(c + 1) * 128], identity=ident_t[:, :]),
                             reads=[yt.k(c * 128, (c + 1) * 128), KC["ident"]], writes=[psk(b)])
                    oa = ob.ap[:, half * 512:(half + 1) * 512]
                    ok_ = ob.k(half * 512, (half + 1) * 512)
                    if half == 0:
                        P.op("dve", lambda e, oa=oa, b=b: e.tensor_copy(out=oa, in_=ps_t[:, b, :]), reads=[psk(b)], writes=[ok_])
                    else:
                        P.op("act", lambda e, oa=oa, b=b: e.activation(out=oa, in_=ps_t[:, b, :], func=AF.Copy), reads=[psk(b)], writes=[ok_])
                    P.ps_release(b)
                r0 = seq * S + tt * 128
                last.append(P.dma("pool", lambda e, ob=ob, r0=r0: e.dma_start(out=out_d[r0:r0 + 128, :], in_=ob.ap), "oout%d" % (tt % 3), reads=[ob.k()]))
            A.pop()
            return last[-3:]

        def attention_layer(layer):
            cfg = LAYER_CFG[layer]
            groups = cfg["groups"] if dbg_groups is None else dbg_groups
            G = len(groups)
            win = win_b[layer]
            wo_d = wout_b[layer]
            k_win = SK["win%d" % layer]
            k_wout = SK["wout%d" % layer]
            win3 = win.rearrange("(k p) n -> p k n", p=128)
            A.push()
            KT = [A.alloc(S, BF16) for _ in range(G)]
            VA = [A.alloc(16 * 128, BF16) for _ in range(G)]
            QT = [A.alloc(S, BF16) for _ in range(2)]
            OT = A.alloc(2 * S, BF16)
            accA = A.alloc(S, F32)
            accB = A.alloc(S, F32)
            NPT = 3
            ptw = max(128 + 2 * w_ for (w_, _d) in groups)
            PT = [[A.alloc(ptw, BF16) for _ in range(2)] for _ in range(NPT)]
            NRB = 3 if G == 1 else 2
            ropebufs = [(A.alloc(TB, BF16), A.alloc(TB, F32), A.alloc(TB, F32)) for _ in range(NRB)]
            wk = [A.alloc(NCH * 128, BF16) for _ in range(G)]
            wv = [A.alloc(NCH * 128, BF16) for _ in range(G)]
            wq = [A.alloc(NCH * 128, BF16) for _ in range(2)]
            wo = A.alloc(2 * D, BF16)
            ropecnt = [0]
            qcnt = [0]
            addcnt = [0]

            def rope_stage_a(it):
                wbuf, tb = it["w"], it["tb"]
                b = P.ps_alloc()
                for k in range(NCH):
                    ha, hk = hs(k, tb * TB, TB)
                    P.op("pe", lambda e, b=b, k=k, ha=ha, wbuf=wbuf: e.matmul(ps_t[:, b, :], lhsT=wbuf.ap[:, k * 128:(k + 1) * 128], rhs=ha,
                                                                               start=(k == 0), stop=(k == NCH - 1)),
                         reads=[wbuf.k(), hk], writes=[psk(b)])
                bufs = ropebufs[ropecnt[0] % NRB]
                ropecnt[0] += 1
                qraw = bufs[0]
                P.op("act", lambda e, b=b, qraw=qraw: e.activation(out=qraw.ap, in_=ps_t[:, b, :], func=AF.Copy), reads=[psk(b)], writes=[qraw.k()])
                P.ps_release(b)
                return bufs

            def rope_stage_b(it, bufs):
                qraw, tt_, uu = bufs
                tb, dst, d = it["tb"], it["dst"], it["d"]
                t0 = tb * TB
                b2 = P.ps_alloc()
                P.op("pe", lambda e: e.matmul(ps_t[:, b2, :], lhsT=rperm_t[:, :], rhs=qraw.ap, start=True, stop=True),
                     reads=[qraw.k(), KC["rperm"]], writes=[psk(b2)])
                P.op("dve", lambda e: e.tensor_tensor(out=tt_.ap, in0=ps_t[:, b2, :], in1=ropes_t[:, t0:t0 + TB], op=ALU.mult),
                     reads=[psk(b2), KC["ropes"]], writes=[tt_.k()])
                P.ps_release(b2)
                P.op("pool", lambda e: e.tensor_tensor(out=uu.ap, in0=qraw.ap, in1=ropec_t[:, t0:t0 + TB], op=ALU.mult),
                     reads=[qraw.k(), KC["ropec"]], writes=[uu.k()])
                if d == 1:
                    da = dst.ap[:, t0:t0 + TB]
                    dk = dst.k(t0, t0 + TB)
                    ia, ib = tt_.ap, uu.ap
                else:
                    da = dst.ap.rearrange("p (r u) -> p r u", r=d)[:, :, t0 // d:(t0 + TB) // d]
                    dk = dst.k()
                    ia = tt_.ap.rearrange("p (u r) -> p r u", r=d)
                    ib = uu.ap.rearrange("p (u r) -> p r u", r=d)
                addcnt[0] += 1
                P.op("dve" if addcnt[0] % 2 == 0 else "pool", lambda e: e.tensor_tensor(out=da, in0=ia, in1=ib, op=ALU.add), reads=[tt_.k(), uu.k()], writes=[dk])

            def proj_rope_many(items):
                pend = None
                for it in list(items) + [None]:
                    cur = None
                    if it is not None:
                        cur = (it, rope_stage_a(it))
                    if pend is not None:
                        rope_stage_b(*pend)
                    pend = cur

            def kv_proj(kp):
                items = []
                for g, (w, d) in enumerate(groups):
                    base = g * 1536
                    kcol = base + 1024 + kp * 128
                    vcol = base + 1280 + kp * 128
                    wload(wk[g].ap.rearrange("p (k n) -> p k n", k=NCH), wk[g].k(), win3[:, :, kcol:kcol + 128], k_win)
                    wload(wv[g].ap.rearrange("p (k n) -> p k n", k=NCH), wv[g].k(), win3[:, :, vcol:vcol + 128], k_win)
                    for tb in range(NTB):
                        items.append(dict(w=wk[g], tb=tb, dst=KT[g], d=d))
                proj_rope_many(items)
                for g, (w, d) in enumerate(groups):
                    Ls = S // d
                    for J in range(16):
                        r = (128 * J) // Ls
                        u0 = (128 * J) % Ls
                        if J % 4 == 0:
                            bv = P.ps_alloc()
                        for k in range(NCH):
                            la, lk = hs(k, r + d * u0, 128, step=d)
                            P.op("pe", lambda e, bv=bv, k=k, la=la, J=J, g=g: e.matmul(ps_t[:, bv, (J % 4) * 128:(J % 4) * 128 + 128], lhsT=la,
                                                                                       rhs=wv[g].ap[:, k * 128:(k + 1) * 128], start=(k == 0), stop=(k == NCH - 1)),
                                 reads=[lk, wv[g].k()], writes=[psk(bv)])
                        if J % 4 == 3:
                            J0 = J - 3
                            va3 = VA[g].ap[:, J0 * 128:(J0 + 4) * 128]
                            P.op("act", lambda e, va3=va3, bv=bv: e.activation(out=va3, in_=ps_t[:, bv, :], func=AF.Copy), reads=[psk(bv)], writes=[VA[g].k(J0 * 128, (J0 + 4) * 128)])
                            P.ps_release(bv)

            def q_proj(unit):
                kp, j, g = unit
                w, d = groups[g]
                hA = 8 * kp + j
                hB = 8 * kp + 4 + j
                qb = QT[qcnt[0] % 2]
                wqb = wq[qcnt[0] % 2]
                qcnt[0] += 1
                wq3 = wqb.ap.rearrange("p (k n) -> p k n", k=NCH)
                wload(wq3[:, :, 0:64], wqb.k(), win3[:, :, g * 1536 + hA * 64:g * 1536 + hA * 64 + 64], k_win)
                wload(wq3[:, :, 64:128], wqb.k(), win3[:, :, g * 1536 + hB * 64:g * 1536 + hB * 64 + 64], k_win)
                proj_rope_many([dict(w=wqb, tb=tb, dst=qb, d=d) for tb in range(NTB)])
                return qb

            def attention_unit(unit, qb):
                kp, j, g = unit
                w, d = groups[g]
                Ls = S // d
                mask_t = m128_t if w == 128 else m64_t
                mmask_t = mm128_t if w == 128 else mm64_t
                tiles_per_sub = Ls // 128
                kt = KT[g]
                va = VA[g]
                rng = {}
                for J in range(16):
                    jl = J % tiles_per_sub
                    sub0 = (J // tiles_per_sub) * Ls
                    qlo = max(0, 128 * jl - w)
                    qhi = min(Ls, 128 * jl + 128 + w)
                    rng[J] = (sub0 + qlo, sub0 + qhi, qlo - (128 * jl - w))
                last_J = {}
                for J in range(16):
                    for m in range(rng[J][0] // 512, (rng[J][1] - 1) // 512 + 1):
                        last_J[m] = J
                acc_banks = {}
                started = set()

                def st_tile(J):
                    glo, ghi, moff = rng[J]
                    nq = ghi - glo
                    pts = PT[J % NPT]
                    banks = [P.ps_alloc(), P.ps_alloc()]
                    for hh in range(2):
                        if MASK_ON_PE[hh]:
                            b = banks[hh]
                            P.op("pe", lambda e, b=b, nq=nq, moff=moff: e.matmul(ps_t[:, b, 0:nq], lhsT=identb_t[:, :], rhs=mask_t[:, moff:moff + nq],
                                                                                start=True, stop=False, skip_group_check=True),
                                 reads=[KC["identb"], KC["m128"], KC["m64"]], writes=[psk(b)])
                    for hh in range(2):
                        rows = slice(hh * 64, hh * 64 + 64)
                        b = banks[hh]
                        P.op("pe", lambda e, b=b, rows=rows, J=J, nq=nq, glo=glo, hh=hh: e.matmul(
                            ps_t[:, b, 0:nq], lhsT=kt.ap[rows, 128 * J:128 * J + 128], rhs=qb.ap[rows, glo:glo + nq],
                            start=(not MASK_ON_PE[hh]), stop=True, skip_group_check=True),
                            reads=[kt.k(128 * J, 128 * J + 128), qb.k(glo, glo + nq)], writes=[psk(b)])
                    for hh in range(2):
                        b = banks[hh]
                        pt = pts[hh]
                        P.op("act", lambda e, b=b, pt=pt, nq=nq: e.activation(out=pt.ap[:, 0:nq], in_=ps_t[:, b, 0:nq], func=AF.Exp, scale=0.125),
                             reads=[psk(b)], writes=[pt.k(0, nq)])
                        P.ps_release(b)
                        if not MASK_ON_PE[hh]:
                            P.op("pool", lambda e, pt=pt, nq=nq, moff=moff: e.tensor_tensor(out=pt.ap[:, 0:nq], in0=pt.ap[:, 0:nq], in1=mmask_t[:, moff:moff + nq], op=ALU.mult),
                                 reads=[pt.k(0, nq), KC["mm"]], writes=[pt.k(0, nq)])

                def pv_tile(J):
                    glo, ghi, _ = rng[J]
                    for m in range(glo // 512, (ghi - 1) // 512 + 1):
                        lo = max(glo, 512 * m)
                        hi = min(ghi, 512 * (m + 1))
                        if m not in acc_banks:
                            acc_banks[m] = (P.ps_alloc(), P.ps_alloc())
                        for (bi, use_v) in ((0, True), (1, False)):
                            bO = acc_banks[m][bi]
                            for hh in range(2):
                                first = (m, bi, hh) not in started
                                started.add((m, bi, hh))
                                pt = PT[J % NPT][hh]
                                lhs = va.ap[:, J * 128 + hh * 64:J * 128 + hh * 64 + 64] if use_v else ones_t[:, 0:64]
                                lk = va.k(J * 128, (J + 1) * 128) if use_v else KC["ones"]
                                P.op("pe", lambda e, bO=bO, pt=pt, lo=lo, hi=hi, glo=glo, m=m, first=first, lhs=lhs, hh=hh: e.matmul(
                                    ps_t[hh * 64:hh * 64 + 64, bO, lo - 512 * m:hi - 512 * m], lhsT=lhs, rhs=pt.ap[:, lo - glo:hi - glo],
                                    start=first, stop=False, skip_group_check=True),
                                    reads=[lk, pt.k(lo - glo, hi - glo)], writes=[psk(bO)])
                        if last_J[m] == J:
                            bA, bB = acc_banks.pop(m)
                            p0 = 512 * m
                            for acc, bO in ((accA, bA), (accB, bB)):
                                if d == 1:
                                    da = acc.ap[:, p0:p0 + 512]
                                    src = ps_t[:, bO, :]
                                elif d == 4:
                                    r = p0 // Ls
                                    da = acc.ap[:, r:S:4]
                                    src = ps_t[:, bO, :]
                                else:
                                    r0 = p0 // Ls
                                    da = acc.ap.rearrange("p (u r) -> p r u", r=16)[:, r0:r0 + 4, :]
                                    src = ps_t[:, bO, :].rearrange("p (r u) -> p r u", r=4)
                                dk = acc.k(p0, p0 + 512) if d == 1 else acc.k()
                                if g == 0:
                                    P.op("act", lambda e, da=da, src=src: e.activation(out=da, in_=src, func=AF.Copy), reads=[psk(bO)], writes=[dk])
                                else:
                                    P.op("dve", lambda e, da=da, src=src: e.tensor_tensor(out=da, in0=da, in1=src, op=ALU.add), reads=[psk(bO), dk], writes=[dk])
                                P.ps_release(bO)

                LAG = 2
                for step in range(16 + LAG):
                    if step < 16:
                        st_tile(step)
                    J = step - LAG
                    if J >= 0:
                        pv_tile(J)

            def normalize(kp, j):
                c = kp * 4 + j
                pl = j % 2
                if cfg["sink"]:
                    P.op("dve", lambda e, c=c: e.tensor_scalar(out=accB.ap, in0=accB.ap, scalar1=es_t[:, c:c + 1], scalar2=None, op0=ALU.add),
                         reads=[accB.k(), KC["es"]], writes=[accB.k()])
                for tb in range(NTB):
                    t0 = tb * TB
                    P.op("dve", lambda e, t0=t0: e.reciprocal(out=accB.ap[:, t0:t0 + TB], in_=accB.ap[:, t0:t0 + TB]), reads=[accB.k(t0, t0 + TB)], writes=[accB.k(t0, t0 + TB)])
                    oa = OT.ap[:, pl * S + t0:pl * S + t0 + TB]
                    P.op("pool", lambda e, oa=oa, t0=t0: e.tensor_tensor(out=oa, in0=accA.ap[:, t0:t0 + TB], in1=accB.ap[:, t0:t0 + TB], op=ALU.mult),
                         reads=[accA.k(t0, t0 + TB), accB.k(t0, t0 + TB)], writes=[OT.k(pl * S + t0, pl * S + t0 + TB)])

            def out_proj(kp, jj):
                wo3 = wo.ap.rearrange("p (j n) -> p j n", j=2)
                for s_ in range(2):
                    j = 2 * jj + s_
                    hA = 8 * kp + j
                    hB = 8 * kp + 4 + j
                    wload(wo3[0:64, s_, :], wo.k(), wo_d[hA * 64:hA * 64 + 64, :], k_wout)
                    wload(wo3[64:128, s_, :], wo.k(), wo_d[hB * 64:hB * 64 + 64, :], k_wout)
                for dc in range(NCH):
                    for tb in range(NTB):
                        b = P.ps_alloc()
                        for j in range(2):
                            P.op("pe", lambda e, b=b, j=j, dc=dc, tb=tb: e.matmul(ps_t[:, b, :], lhsT=wo.ap[:, j * D + dc * 128:j * D + dc * 128 + 128],
                                                                                   rhs=OT.ap[:, j * S + tb * TB:j * S + (tb + 1) * TB], start=(j == 0), stop=(j == 1)),
                                 reads=[wo.k(), OT.k(j * S + tb * TB, j * S + (tb + 1) * TB)], writes=[psk(b)])
                        xa, xk = xs(dc, tb * TB, TB)
                        P.op("dve", lambda e, xa=xa, b=b: e.tensor_tensor(out=xa, in0=xa, in1=ps_t[:, b, :], op=ALU.add), reads=[xk, psk(b)], writes=[xk])
                        P.ps_release(b)

            kv_proj(0)
            for kp in range(2):
                if kp == 1 and layer == 0:
                    do_casts("attn0")
                units = [(kp, j, g) for j in range(4) for g in range(G)]
                qb_next = q_proj(units[0])
                for i, unit in enumerate(units):
                    qb_cur = qb_next
                    if i + 1 < len(units):
                        qb_next = q_proj(units[i + 1])
                    attention_unit(unit, qb_cur)
                    if unit[2] == G - 1:
                        j = unit[1]
                        normalize(kp, j)
                        if j % 2 == 1:
                            if j == 3 and kp + 1 < 2:
                                kv_proj(kp + 1)
                            out_proj(kp, j // 2)
            A.pop()

        def ffn_layer(layer, after_half=None):
            A.push()
            TH = 1024
            actT = A.alloc(NFC * TH, BF16)
            wgb = [A.alloc(NCH * 256, BF16) for _ in range(2)]
            wub = [A.alloc(NCH * 256, BF16) for _ in range(2)]
            wdb = [A.alloc(NFC * 128, BF16) for _ in range(2)]
            sg = [A.alloc(TB, F32) for _ in range(2)]
            cnt = 0
            if layer == 0:
                do_casts("ffn0")
            for half in range(S // TH):
                for fp in range(NFC // 2):
                    gb, ub = wgb[fp % 2], wub[fp % 2]
                    wload(gb.ap.rearrange("p (k n) -> p k n", k=NCH), gb.k(), wg_b[layer].rearrange("(k p) n -> p k n", p=128)[:, :, fp * 256:(fp + 1) * 256], SK["wg%d" % layer])
                    wload(ub.ap.rearrange("p (k n) -> p k n", k=NCH), ub.k(), wu_b[layer].rearrange("(k p) n -> p k n", p=128)[:, :, fp * 256:(fp + 1) * 256], SK["wu%d" % layer])
                    for fc in range(2):
                        f = fp * 2 + fc
                        for tbl in range(TH // TB):
                            tb = half * (TH // TB) + tbl
                            bg = P.ps_alloc()
                            bu = P.ps_alloc()
                            for (wb, b) in ((gb, bg), (ub, bu)):
                                for k in range(NCH):
                                    ha, hk = hs(k, tb * TB, TB)
                                    P.op("pe", lambda e, wb=wb, b=b, k=k, ha=ha, fc=fc: e.matmul(ps_t[:, b, :], lhsT=wb.ap[:, k * 256 + fc * 128:k * 256 + fc * 128 + 128], rhs=ha,
                                                                                                 start=(k == 0), stop=(k == NCH - 1)),
                                         reads=[wb.k(), hk], writes=[psk(b)])
                            s_ = sg[cnt % 2]
                            cnt += 1
                            P.op("act", lambda e, s_=s_, bg=bg: e.activation(out=s_.ap, in_=ps_t[:, bg, :], func=AF.Silu), reads=[psk(bg)], writes=[s_.k()])
                            P.ps_release(bg)
                            aa = actT.ap[:, f * TH + tbl * TB:f * TH + (tbl + 1) * TB]
                            ak = actT.k(f * TH + tbl * TB, f * TH + (tbl + 1) * TB)
                            P.op("dve", lambda e, aa=aa, s_=s_, bu=bu: e.tensor_tensor(out=aa, in0=s_.ap, in1=ps_t[:, bu, :], op=ALU.mult), reads=[s_.k(), psk(bu)], writes=[ak])
                            P.ps_release(bu)
                for dc in range(NCH):
                    db = wdb[dc % 2]
                    wload(db.ap.rearrange("p (k n) -> p k n", k=NFC), db.k(), wd_b[layer].rearrange("(k p) n -> p k n", p=128)[:, :, dc * 128:(dc + 1) * 128], SK["wd%d" % layer])
                    for tbl in range(TH // TB):
                        tb = half * (TH // TB) + tbl
                        b = P.ps_alloc()
                        for f in range(NFC):
                            P.op("pe", lambda e, b=b, f=f, db=db, tbl=tbl: e.matmul(ps_t[:, b, :], lhsT=db.ap[:, f * 128:(f + 1) * 128],
                                                                                     rhs=actT.ap[:, f * TH + tbl * TB:f * TH + (tbl + 1) * TB], start=(f == 0), stop=(f == NFC - 1)),
                                 reads=[db.k(), actT.k(f * TH + tbl * TB, f * TH + (tbl + 1) * TB)], writes=[psk(b)])
                        xa, xk = xs(dc, tb * TB, TB)
                        P.op("dve", lambda e, xa=xa, b=b: e.tensor_tensor(out=xa, in0=xa, in1=ps_t[:, b, :], op=ALU.add), reads=[xk, psk(b)], writes=[xk])
                        P.ps_release(b)
                if after_half is not None:
                    after_half(half)
            A.pop()

        finals = []
        layers = list(layer_list if layer_list is not None else range(nlayers))
        for seq in range(nseq):
            load_x(seq)
            do_casts("after_load")
            A.push()
            rstd_f = None
            pre_normed = False
            for li, layer in enumerate(layers):
                if do_attn:
                    if not pre_normed:
                        rmsnorm_to_hT(layer)
                    attention_layer(layer)
                pre_normed = False
                if do_ffn:
                    rmsnorm_to_hT(2 + layer)
                    cb = None
                    if li + 1 < len(layers) and do_attn:
                        nxt = layers[li + 1]
                        cb = lambda half, nxt=nxt: rmsnorm_to_hT(nxt, [2 * half, 2 * half + 1])
                        pre_normed = True
                    elif li + 1 == len(layers) and final_norm:
                        rstd_f = A.alloc(S, F32)
                        sq_f = [A.alloc(TB, BF16) for _ in range(2)]
                        cb = lambda half: norm_stats(rstd_f, [2 * half, 2 * half + 1], sq_f)
                    ffn_layer(layer, cb)
            finals += store_out(seq, 4 if final_norm else None, rstd_f)
            A.pop()
        counts = P.build(final_wait_ops=finals)
        counts['arena_hw'] = getattr(A, 'hw', 0)
        counts['arena'] = ARENA_BYTES
    return nc, counts, len(P.ops)


def _consts():
    inv_freq = (1.0 / (np.float32(10000.0) ** (np.arange(0, 64, 2, dtype=np.float32) / np.float32(64)))).astype(np.float32)
    ang = (np.arange(S, dtype=np.float32)[:, None] * inv_freq[None, :]).astype(np.float32)
    cos = np.cos(ang).astype(np.float32).T
    sin = np.sin(ang).astype(np.float32).T
    ropec = np.tile(cos, (4, 1)).astype(np.float32)
    sgn = np.where((np.arange(128) % 64) < 32, -1.0, 1.0).astype(np.float32)[:, None]
    ropes = (np.tile(sin, (4, 1)) * sgn).astype(np.float32)
    ident = np.eye(128, dtype=np.float32)
    rperm = np.zeros((128, 128), np.float32)
    for m in range(128):
        partner = m + 32 if (m % 64) < 32 else m - 32
        rperm[partner, m] = 1.0
    b = np.arange(128)[:, None]
    a128 = np.arange(384)[None, :]
    m128 = (((b <= a128) & (a128 <= b + 256)).astype(np.float32) - 1.0) * 30000.0
    a64 = np.arange(256)[None, :]
    m64 = (((b <= a64) & (a64 <= b + 128)).astype(np.float32) - 1.0) * 30000.0
    mm128 = ((b <= a128) & (a128 <= b + 256)).astype(np.float32)
    mm64 = ((b <= a64) & (a64 <= b + 128)).astype(np.float32)
    return dict(ropec=ropec, ropes=ropes, ident=ident, rperm=rperm, m128=m128, m64=m64, mm128=mm128, mm64=mm64)


_CACHE = {}


def _prep_shared(a_w_in, a_sink, a_w_out, b_w_in, b_w_out, norm_mix, norm_ffn, w_gate, w_up, w_down, final_norm):
    f = lambda v: np.ascontiguousarray(np.asarray(v, dtype=np.float32))
    gains_src = [f(norm_mix)[0], f(norm_mix)[1], f(norm_ffn)[0], f(norm_ffn)[1], f(final_norm)]
    gains = np.concatenate([g.reshape(8, 128).T for g in gains_src], axis=1)
    sk = f(a_sink)[0]
    colsA = [8 * kp + j for kp in range(2) for j in range(4)]
    colsB = [8 * kp + 4 + j for kp in range(2) for j in range(4)]
    sinkb = np.concatenate([np.broadcast_to(sk[colsA][None, :], (64, 8)), np.broadcast_to(sk[colsB][None, :], (64, 8))], axis=0)
    d = dict(a_w_in=f(a_w_in)[0], a_w_out=f(a_w_out)[0], b_w_in=f(b_w_in)[0], b_w_out=f(b_w_out)[0],
             w_gate=f(w_gate), w_up=f(w_up), w_down=f(w_down),
             gains=np.ascontiguousarray(gains), sinkb=np.ascontiguousarray(sinkb))
    d.update(_consts())
    return d


def kernel(x, a_w_in, a_sink, a_w_out, b_w_in, b_w_out, norm_mix, norm_ffn, w_gate, w_up, w_down, final_norm):
    x = np.asarray(x, dtype=np.float32)
    B = x.shape[0]
    per = B // NCORES
    if "nc" not in _CACHE:
        _CACHE["nc"] = build_program(per, 2, True)[0]
    nc = _CACHE["nc"]
    shared = _prep_shared(a_w_in, a_sink, a_w_out, b_w_in, b_w_out, norm_mix, norm_ffn, w_gate, w_up, w_down, final_norm)
    in_maps = []
    for i in range(NCORES):
        m = dict(shared)
        m["x"] = np.ascontiguousarray(x[i * per:(i + 1) * per].reshape(per * S, D))
        in_maps.append(m)
    res = run_bass_kernel_spmd(nc, in_maps, core_ids=list(range(NCORES)))
    out = np.concatenate([np.asarray(r["out"], dtype=np.float32).reshape(per, S, D) for r in res.results], axis=0)
    return out
```

```python
import contextlib
import numpy as np
import concourse.bass as bass
import concourse.mybir as mybir
from concourse.bass_utils import run_bass_kernel_spmd

F32 = mybir.dt.float32
BF16 = mybir.dt.bfloat16
AF = mybir.ActivationFunctionType
ALU = mybir.AluOpType

S = 2048
D = 1024
DFF = 2816
NCH = 8
NFC = 22
TB = 512
NTB = 4
NCORES = 8
SEQ_PER_CORE = 4
ENGS = ("pe", "act", "dve", "pool", "sp")
SAME_ENGINE_SYNC = True
MASK_ON_PE = (True, True)


class Prog:
    def __init__(self, nc):
        self.nc = nc
        self.ops = []
        self.iv = {}
        self.dma_slots = {}
        self.ps_free = list(range(8))

    def _access(self, idx, eng_id, key, is_write, deps):
        space, lo, hi = key
        lst = self.iv.setdefault(space, [])
        new = []
        for rec in lst:
            l, h, i, w, e = rec
            if l < hi and lo < h:
                if is_write or w:
                    deps.add(i)
                if is_write and lo <= l and h <= hi:
                    continue
                if (not is_write) and (not w) and e == eng_id and l == lo and h == hi:
                    continue
            new.append(rec)
        new.append((lo, hi, idx, is_write, eng_id))
        self.iv[space] = new

    def _record(self, eng, emit, reads, writes, dma):
        idx = len(self.ops)
        deps = set()
        eng_id = eng if dma is None else ("dma", idx)
        for k in reads:
            self._access(idx, eng_id, k, False, deps)
        for k in writes:
            self._access(idx, eng_id, k, True, deps)
        deps.discard(idx)
        self.ops.append(dict(eng=eng, emit=emit, deps=deps, dma=dma, signal=False))
        return idx

    def op(self, eng, emit, reads=(), writes=()):
        return self._record(eng, emit, reads, writes, None)

    def dma(self, queue, emit, slot, reads=(), writes=()):
        cnt = self.dma_slots.get(slot, 0) + 16
        self.dma_slots[slot] = cnt
        return self._record(queue, emit, reads, writes, (slot, cnt))

    def ps_alloc(self):
        assert self.ps_free, "out of PSUM banks"
        return self.ps_free.pop(0)

    def ps_release(self, b):
        self.ps_free.append(b)

    def build(self, final_wait_ops=()):
        nc = self.nc
        ops = self.ops
        def needs_edge(x, o):
            if x["dma"] is not None:
                return True
            if o["dma"] is not None:
                return True
            if x["eng"] != o["eng"]:
                return True
            return SAME_ENGINE_SYNC and x["eng"] != "pe"

        for o in ops:
            for d in o["deps"]:
                x = ops[d]
                if x["dma"] is None and needs_edge(x, o):
                    x["signal"] = True
        counts = {e: 0 for e in ENGS}
        for o in ops:
            if o["dma"] is None and o["signal"]:
                counts[o["eng"]] += 1
                o["cnt"] = counts[o["eng"]]
        with contextlib.ExitStack() as st:
            esem = {e: st.enter_context(nc.semaphore("s_" + e)) for e in ENGS}
            dsem = {s: st.enter_context(nc.semaphore("d_%s" % (s,))) for s in self.dma_slots}
            block = st.enter_context(nc.Block())
            per_eng = {e: [] for e in ENGS}
            for i, o in enumerate(ops):
                per_eng[o["eng"]].append(i)

            def run(eng_name, eng):
                waited = {}
                for i in per_eng[eng_name]:
                    o = ops[i]
                    need = {}
                    for d in o["deps"]:
                        x = ops[d]
                        if x["dma"] is not None:
                            key = ("d", x["dma"][0])
                            val = x["dma"][1]
                        else:
                            if not needs_edge(x, o):
                                continue
                            key = ("e", x["eng"])
                            val = x["cnt"]
                        if val > need.get(key, 0):
                            need[key] = val
                    for key, val in need.items():
                        if waited.get(key, 0) >= val:
                            continue
                        waited[key] = val
                        sem = dsem[key[1]] if key[0] == "d" else esem[key[1]]
                        eng.wait_ge(sem, val)
                    ins = o["emit"](eng)
                    if o["dma"] is not None:
                        ins.then_inc(dsem[o["dma"][0]], 16)
                    elif o["signal"]:
                        ins.then_inc(esem[eng_name], 1)
                if eng_name == "sp":
                    for i in final_wait_ops:
                        x = ops[i]
                        eng.wait_ge(dsem[x["dma"][0]], x["dma"][1])

            @block.tensor
            def _(e):
                run("pe", e)

            @block.scalar
            def _(e):
                run("act", e)

            @block.vector
            def _(e):
                run("dve", e)

            @block.gpsimd
            def _(e):
                run("pool", e)

            @block.sync
            def _(e):
                run("sp", e)
        return counts


class Buf:
    def __init__(self, space, ap2d, byte_off, dt):
        self.space = space
        self.ap = ap2d
        self.off = byte_off
        self.esz = 2 if dt == BF16 else 4

    def k(self, lo=0, hi=None):
        if hi is None:
            hi = self.ap.shape[1]
        return (self.space, self.off + lo * self.esz, self.off + hi * self.esz)


class Arena:
    def __init__(self, t, nbytes):
        self.t = t
        self.nbytes = nbytes
        self.cur = 0
        self.marks = []

    def alloc(self, nelem, dt):
        esz = 2 if dt == BF16 else 4
        nb = (nelem * esz + 63) // 64 * 64
        off = self.cur
        assert off + nb <= self.nbytes, ("arena overflow", off + nb, self.nbytes)
        self.cur += nb
        self.hw = max(getattr(self, 'hw', 0), self.cur)
        v = self.t[:, off // 4:(off + nb) // 4]
        if dt == BF16:
            v = v.bitcast(BF16)
        v = v[:, 0:nelem]
        return Buf("arena", v, off, dt)

    def push(self):
        self.marks.append(self.cur)

    def pop(self):
        self.cur = self.marks.pop()


LAYER_CFG = [
    dict(groups=[(128, 1)], sink=True),
    dict(groups=[(64, 1), (64, 4), (64, 16)], sink=False),
]


def build_program(nseq=SEQ_PER_CORE, nlayers=2, final_norm=True, do_attn=True, do_ffn=True, layer_list=None, dbg_groups=None):
    nc = bass.Bass("TRN2", target_bir_lowering=False)
    dt_in = lambda name, shape: nc.dram_tensor(name, shape, F32, kind="ExternalInput").ap()
    x_d = dt_in("x", [nseq * S, D])
    win_d = [dt_in("a_w_in", [D, 1536]), dt_in("b_w_in", [D, 4608])]
    wout_d = [dt_in("a_w_out", [D, D]), dt_in("b_w_out", [D, D])]
    wg_d = dt_in("w_gate", [2, D, DFF])
    wu_d = dt_in("w_up", [2, D, DFF])
    wd_d = dt_in("w_down", [2, DFF, D])
    gains_d = dt_in("gains", [128, 40])
    sinkb_d = dt_in("sinkb", [128, 8])
    ropec_d = dt_in("ropec", [128, S])
    ropes_d = dt_in("ropes", [128, S])
    ident_d = dt_in("ident", [128, 128])
    rperm_d = dt_in("rperm", [128, 128])
    m128_d = dt_in("m128", [128, 384])
    m64_d = dt_in("m64", [128, 256])
    mm128_d = dt_in("mm128", [128, 384])
    mm64_d = dt_in("mm64", [128, 256])
    out_d = nc.dram_tensor("out", [nseq * S, D], F32, kind="ExternalOutput").ap()

    st = contextlib.ExitStack()
    with st:
        def sbt(name, shape, dt):
            return st.enter_context(nc.sbuf_tensor("sb_" + name, shape, dt))

        xT_t = sbt("xT", [128, NCH * S], F32)
        hT_t = sbt("hT", [128, NCH * S], BF16)
        ropec_t = sbt("ropec", [128, S], F32)
        ropes_t = sbt("ropes", [128, S], F32)
        ident_t = sbt("identf", [128, 128], F32)
        identb_t = sbt("identb", [128, 128], BF16)
        rperm_t = sbt("rperm", [128, 128], BF16)
        ones_t = sbt("ones", [128, 128], BF16)
        m128_t = sbt("m128", [128, 384], BF16)
        m64_t = sbt("m64", [128, 256], BF16)
        mm128_t = sbt("mm128", [128, 384], BF16)
        mm64_t = sbt("mm64", [128, 256], BF16)
        gains_t = sbt("gains", [128, 40], F32)
        es_t = sbt("es", [128, 8], F32)
        ARENA_BYTES = 93952
        arena_t = sbt("arena", [128, ARENA_BYTES // 4], F32)
        ps_t = st.enter_context(nc.psum_tensor("ps", [128, 8, 512], F32))

        P = Prog(nc)
        A = Arena(arena_t, ARENA_BYTES)
        xT = Buf("xT", xT_t[:, :], 0, F32)
        hT = Buf("hT", hT_t[:, :], 0, BF16)
        CONST = ("const", 0, 1)
        CONSTS = None
        KC = {n: ("c_" + n, 0, 1) for n in ("ropec", "ropes", "ident", "gains", "es", "rperm", "m128", "m64", "ones", "identb", "mm")}

        def psk(b):
            return ("ps", b, b + 1)

        def xs(c, t0, n):
            return xT_t[:, c * S + t0:c * S + t0 + n], xT.k(c * S + t0, c * S + t0 + n)

        def hs(c, t0, n, step=1):
            end = t0 + (n - 1) * step + 1
            return hT_t[:, c * S + t0:c * S + end:step], hT.k(c * S + t0, c * S + end)

        P.dma("sp", lambda e: e.dma_start(out=ropec_t[:, :], in_=ropec_d), "c0", writes=[KC["ropec"]])
        P.dma("sp", lambda e: e.dma_start(out=ropes_t[:, :], in_=ropes_d), "c1", writes=[KC["ropes"]])
        P.dma("sp", lambda e: e.dma_start(out=ident_t[:, :], in_=ident_d), "c2", writes=[KC["ident"]])
        P.dma("sp", lambda e: e.dma_start(out=gains_t[:, :], in_=gains_d), "c3", writes=[KC["gains"]])
        P.dma("sp", lambda e: e.dma_start(out=es_t[:, :], in_=sinkb_d), "c4", writes=[KC["es"]])
        P.dma("pool", lambda e: e.dma_start(out=rperm_t[:, :], in_=rperm_d), "c5", writes=[KC["rperm"]])
        P.dma("pool", lambda e: e.dma_start(out=m128_t[:, :], in_=m128_d), "c6", writes=[KC["m128"]])
        P.dma("pool", lambda e: e.dma_start(out=m64_t[:, :], in_=m64_d), "c7", writes=[KC["m64"]])
        P.dma("pool", lambda e: e.dma_start(out=identb_t[:, :], in_=ident_d), "c8", writes=[KC["identb"]])
        P.dma("pool", lambda e: e.dma_start(out=mm128_t[:, :], in_=mm128_d), "c9", writes=[KC["mm"]])
        P.dma("pool", lambda e: e.dma_start(out=mm64_t[:, :], in_=mm64_d), "c10", writes=[KC["mm"]])
        P.op("dve", lambda e: e.memset(ones_t[:, :], 1.0), writes=[KC["ones"]])
        P.op("act", lambda e: e.activation(out=es_t[:, :], in_=es_t[:, :], func=AF.Exp), reads=[KC["es"]], writes=[KC["es"]])

        wslot = [0]
        def scratch(name, shape):
            return nc.dram_tensor("sc_" + name, shape, BF16).ap()
        win_b = [scratch("a_w_in", [D, 1536]), scratch("b_w_in", [D, 4608])]
        wout_b = [scratch("a_w_out", [D, D]), scratch("b_w_out", [D, D])]
        wg_b = scratch("w_gate", [2, D, DFF])
        wu_b = scratch("w_up", [2, D, DFF])
        wd_b = scratch("w_down", [2, DFF, D])
        SK = {}

        def cast1(name, dst, src):
            P.dma("pool", lambda e: e.dma_start(out=dst, in_=src), "cast_" + name, writes=[SK[name]])

        cast_plan = {
            "start": [("win0", win_b[0], win_d[0]), ("wout0", wout_b[0], wout_d[0])],
            "after_load": [("wg0", wg_b[0], wg_d[0]), ("wu0", wu_b[0], wu_d[0]), ("wd0", wd_b[0], wd_d[0])],
            "attn0": [("win1", win_b[1], win_d[1]), ("wout1", wout_b[1], wout_d[1])],
            "ffn0": [("wg1", wg_b[1], wg_d[1]), ("wu1", wu_b[1], wu_d[1]), ("wd1", wd_b[1], wd_d[1])],
        }
        for lst in cast_plan.values():
            for (name, _d, _s) in lst:
                SK[name] = ("sc_" + name, 0, 1)

        def do_casts(tag):
            for (name, dst, src) in cast_plan.pop(tag, []):
                cast1(name, dst, src)

        do_casts("start")

        def wload(dst_ap, dst_key, src_ap, src_key):
            wslot[0] += 1
            P.dma("sp", lambda e: e.dma_start(out=dst_ap, in_=src_ap), "w%d" % (wslot[0] % 40), reads=[src_key], writes=[dst_key])

        def norm_stats(rstd, tbs, sq):
            for tb in tbs:
                b = P.ps_alloc()
                for c in range(NCH):
                    xa, xk = xs(c, tb * TB, TB)
                    sb_ = sq[(tb * NCH + c) % len(sq)]
                    if c % 2 == 0:
                        P.op("act", lambda e, sb_=sb_, xa=xa: e.activation(out=sb_.ap, in_=xa, func=AF.Square),
                             reads=[xk], writes=[sb_.k()])
                    else:
                        P.op("dve", lambda e, sb_=sb_, xa=xa: e.tensor_tensor(out=sb_.ap, in0=xa, in1=xa, op=ALU.mult),
                             reads=[xk], writes=[sb_.k()])
                    P.op("pe", lambda e, sb_=sb_, b=b, c=c: e.matmul(ps_t[:, b, :], lhsT=ones_t[:, :], rhs=sb_.ap,
                                                                      start=(c == 0), stop=(c == NCH - 1)),
                         reads=[sb_.k(), KC["ones"]], writes=[psk(b)])
                ra = rstd.ap[:, tb * TB:(tb + 1) * TB]
                rk = rstd.k(tb * TB, (tb + 1) * TB)
                P.op("act", lambda e, ra=ra, b=b: e.activation(out=ra, in_=ps_t[:, b, :], func=AF.Sqrt, bias=1e-6, scale=1.0 / D),
                     reads=[psk(b)], writes=[rk])
                P.ps_release(b)
                P.op("dve", lambda e, ra=ra: e.reciprocal(out=ra, in_=ra), reads=[rk], writes=[rk])

        def rmsnorm_to_hT(gain_idx, tbs=None):
            tbs = list(range(NTB)) if tbs is None else list(tbs)
            A.push()
            rstd = A.alloc(S, F32)
            sq = [A.alloc(TB, BF16) for _ in range(3)]
            ntmp = [A.alloc(TB, F32) for _ in range(2)]
            norm_stats(rstd, tbs, sq)
            for tb in tbs:
                ra = rstd.ap[:, tb * TB:(tb + 1) * TB]
                rk = rstd.k(tb * TB, (tb + 1) * TB)
                for c in range(NCH):
                    xa, xk = xs(c, tb * TB, TB)
                    ha, hk = hs(c, tb * TB, TB)
                    if c % 2 == 0:
                        P.op("dve", lambda e, ha=ha, xa=xa, ra=ra, c=c: e.scalar_tensor_tensor(
                            out=ha, in0=xa, scalar=gains_t[:, gain_idx * 8 + c:gain_idx * 8 + c + 1], in1=ra,
                            op0=ALU.mult, op1=ALU.mult), reads=[xk, rk, KC["gains"]], writes=[hk])
                    else:
                        tm = ntmp[(c // 2) % 2]
                        P.op("pool", lambda e, tm=tm, xa=xa, ra=ra: e.tensor_tensor(out=tm.ap, in0=xa, in1=ra, op=ALU.mult), reads=[xk, rk], writes=[tm.k()])
                        P.op("act", lambda e, tm=tm, ha=ha, c=c: e.activation(out=ha, in_=tm.ap, func=AF.Copy, scale=gains_t[:, gain_idx * 8 + c:gain_idx * 8 + c + 1]),
                             reads=[tm.k(), KC["gains"]], writes=[hk])
            A.pop()

        def load_x(seq):
            A.push()
            A.alloc(12288, F32)
            xin = [A.alloc(D, F32) for _ in range(4)]
            for tt in range(S // 128):
                xb = xin[tt % 4]
                r0 = seq * S + tt * 128
                P.dma("sp", lambda e, xb=xb, r0=r0: e.dma_start(out=xb.ap, in_=x_d[r0:r0 + 128, :]), "xin%d" % (tt % 4), writes=[xb.k()])
                for half in range(2):
                    b = P.ps_alloc()
                    for j in range(4):
                        c = half * 4 + j
                        P.op("pe", lambda e, xb=xb, b=b, j=j, c=c: e.transpose(out=ps_t[:, b, j * 128:(j + 1) * 128],
                                                                                 in_=xb.ap[:, c * 128:(c + 1) * 128], identity=ident_t[:, :]),
                             reads=[xb.k(), KC["ident"]], writes=[psk(b)])
                    dst = xT_t[:, :].rearrange("p (c t) -> p c t", c=NCH)[:, half * 4:half * 4 + 4, tt * 128:(tt + 1) * 128]
                    src = ps_t[:, b, :].rearrange("p (c t) -> p c t", c=4)
                    keys = [xT.k(c * S + tt * 128, c * S + tt * 128 + 128) for c in range(half * 4, half * 4 + 4)]
                    eng = "dve" if half == 0 else "act"
                    if eng == "dve":
                        P.op("dve", lambda e, dst=dst, src=src: e.tensor_copy(out=dst, in_=src), reads=[psk(b)], writes=keys)
                    else:
                        P.op("act", lambda e, dst=dst, src=src: e.activation(out=dst, in_=src, func=AF.Copy), reads=[psk(b)], writes=keys)
                    P.ps_release(b)
            A.pop()

        def store_out(seq, gain_idx, rstd=None):
            A.push()
            ost = [A.alloc(D, F32) for _ in range(3)]
            ytmp = [A.alloc(128 * NCH, F32) for _ in range(3)]
            last = []
            if gain_idx is not None and rstd is None:
                rstd = A.alloc(S, F32)
                sq = [A.alloc(TB, BF16) for _ in range(3)]
                norm_stats(rstd, range(NTB), sq)
            for tt in range(S // 128):
                yt = ytmp[tt % 3]
                ob = ost[tt % 3]
                for c in range(NCH):
                    xa, xk = xs(c, tt * 128, 128)
                    ya = yt.ap[:, c * 128:(c + 1) * 128]
                    yk = yt.k(c * 128, (c + 1) * 128)
                    eng = "dve" if (c % 2 == 0 or gain_idx is not None) else "pool"
                    if gain_idx is not None:
                        ra = rstd.ap[:, tt * 128:(tt + 1) * 128]
                        rk = rstd.k(tt * 128, (tt + 1) * 128)
                        P.op(eng, lambda e, ya=ya, xa=xa, ra=ra, c=c: e.scalar_tensor_tensor(
                            out=ya, in0=xa, scalar=gains_t[:, gain_idx * 8 + c:gain_idx * 8 + c + 1], in1=ra, op0=ALU.mult, op1=ALU.mult),
                            reads=[xk, rk, KC["gains"]], writes=[yk])
                    else:
                        P.op(eng, lambda e, ya=ya, xa=xa: e.tensor_copy(out=ya, in_=xa), reads=[xk], writes=[yk])
                for half in range(2):
                    b = P.ps_alloc()
                    for j in range(4):
                        c = half * 4 + j
                        P.op("pe", lambda e, yt=yt, b=b, j=j, c=c: e.transpose(out=ps_t[:, b, j * 128:(j + 1) * 128], in_=yt.ap[:, c * 128:(c + 1) * 128], identity=ident_t[:, :]),
                             reads=[yt.k(c * 128, (c + 1) * 128), KC["ident"]], writes=[psk(b)])
                    oa = ob.ap[:, half * 512:(half + 1) * 512]
                    ok_ = ob.k(half * 512, (half + 1) * 512)
                    if half == 0:
                        P.op("dve", lambda e, oa=oa, b=b: e.tensor_copy(out=oa, in_=ps_t[:, b, :]), reads=[psk(b)], writes=[ok_])
                    else:
                        P.op("act", lambda e, oa=oa, b=b: e.activation(out=oa, in_=ps_t[:, b, :], func=AF.Copy), reads=[psk(b)], writes=[ok_])
                    P.ps_release(b)
                r0 = seq * S + tt * 128
                last.append(P.dma("pool", lambda e, ob=ob, r0=r0: e.dma_start(out=out_d[r0:r0 + 128, :], in_=ob.ap), "oout%d" % (tt % 3), reads=[ob.k()]))
            A.pop()
            return last[-3:]

        def attention_layer(layer):
            cfg = LAYER_CFG[layer]
            groups = cfg["groups"] if dbg_groups is None else dbg_groups
            G = len(groups)
            win = win_b[layer]
            wo_d = wout_b[layer]
            k_win = SK["win%d" % layer]
            k_wout = SK["wout%d" % layer]
            win3 = win.rearrange("(k p) n -> p k n", p=128)
            A.push()
            KT = [A.alloc(S, BF16) for _ in range(G)]
            VA = [A.alloc(16 * 128, BF16) for _ in range(G)]
            QT = [A.alloc(S, BF16) for _ in range(2)]
            OT = A.alloc(2 * S, BF16)
            accA = A.alloc(S, F32)
            accB = A.alloc(S, F32)
            NPT = 3
            ptw = max(128 + 2 * w_ for (w_, _d) in groups)
            PT = [[A.alloc(ptw, BF16) for _ in range(2)] for _ in range(NPT)]
            NRB = 3 if G == 1 else 2
            ropebufs = [(A.alloc(TB, BF16), A.alloc(TB, F32), A.alloc(TB, F32)) for _ in range(NRB)]
            wk = [A.alloc(NCH * 128, BF16) for _ in range(G)]
            wv = [A.alloc(NCH * 128, BF16) for _ in range(G)]
            wq = [A.alloc(NCH * 128, BF16) for _ in range(2)]
            wo = A.alloc(2 * D, BF16)
            ropecnt = [0]
            qcnt = [0]
            addcnt = [0]

            def rope_stage_a(it):
                wbuf, tb = it["w"], it["tb"]
                b = P.ps_alloc()
                for k in range(NCH):
                    ha, hk = hs(k, tb * TB, TB)
                    P.op("pe", lambda e, b=b, k=k, ha=ha, wbuf=wbuf: e.matmul(ps_t[:, b, :], lhsT=wbuf.ap[:, k * 128:(k + 1) * 128], rhs=ha,
                                                                               start=(k == 0), stop=(k == NCH - 1)),
                         reads=[wbuf.k(), hk], writes=[psk(b)])
                bufs = ropebufs[ropecnt[0] % NRB]
                ropecnt[0] += 1
                qraw = bufs[0]
                P.op("act", lambda e, b=b, qraw=qraw: e.activation(out=qraw.ap, in_=ps_t[:, b, :], func=AF.Copy), reads=[psk(b)], writes=[qraw.k()])
                P.ps_release(b)
                return bufs

            def rope_stage_b(it, bufs):
                qraw, tt_, uu = bufs
                tb, dst, d = it["tb"], it["dst"], it["d"]
                t0 = tb * TB
                b2 = P.ps_alloc()
                P.op("pe", lambda e: e.matmul(ps_t[:, b2, :], lhsT=rperm_t[:, :], rhs=qraw.ap, start=True, stop=True),
                     reads=[qraw.k(), KC["rperm"]], writes=[psk(b2)])
                P.op("dve", lambda e: e.tensor_tensor(out=tt_.ap, in0=ps_t[:, b2, :], in1=ropes_t[:, t0:t0 + TB], op=ALU.mult),
                     reads=[psk(b2), KC["ropes"]], writes=[tt_.k()])
                P.ps_release(b2)
                P.op("pool", lambda e: e.tensor_tensor(out=uu.ap, in0=qraw.ap, in1=ropec_t[:, t0:t0 + TB], op=ALU.mult),
                     reads=[qraw.k(), KC["ropec"]], writes=[uu.k()])
                if d == 1:
                    da = dst.ap[:, t0:t0 + TB]
                    dk = dst.k(t0, t0 + TB)
                    ia, ib = tt_.ap, uu.ap
                else:
                    da = dst.ap.rearrange("p (r u) -> p r u", r=d)[:, :, t0 // d:(t0 + TB) // d]
                    dk = dst.k()
                    ia = tt_.ap.rearrange("p (u r) -> p r u", r=d)
                    ib = uu.ap.rearrange("p (u r) -> p r u", r=d)
                addcnt[0] += 1
                P.op("dve" if addcnt[0] % 2 == 0 else "pool", lambda e: e.tensor_tensor(out=da, in0=ia, in1=ib, op=ALU.add), reads=[tt_.k(), uu.k()], writes=[dk])

            def proj_rope_many(items):
                pend = None
                for it in list(items) + [None]:
                    cur = None
                    if it is not None:
                        cur = (it, rope_stage_a(it))
                    if pend is not None:
                        rope_stage_b(*pend)
                    pend = cur

            def kv_proj(kp):
                items = []
                for g, (w, d) in enumerate(groups):
                    base = g * 1536
                    kcol = base + 1024 + kp * 128
                    vcol = base + 1280 + kp * 128
                    wload(wk[g].ap.rearrange("p (k n) -> p k n", k=NCH), wk[g].k(), win3[:, :, kcol:kcol + 128], k_win)
                    wload(wv[g].ap.rearrange("p (k n) -> p k n", k=NCH), wv[g].k(), win3[:, :, vcol:vcol + 128], k_win)
                    for tb in range(NTB):
                        items.append(dict(w=wk[g], tb=tb, dst=KT[g], d=d))
                proj_rope_many(items)
                for g, (w, d) in enumerate(groups):
                    Ls = S // d
                    for J in range(16):
                        r = (128 * J) // Ls
                        u0 = (128 * J) % Ls
                        if J % 4 == 0:
                            bv = P.ps_alloc()
                        for k in range(NCH):
                            la, lk = hs(k, r + d * u0, 128, step=d)
                            P.op("pe", lambda e, bv=bv, k=k, la=la, J=J, g=g: e.matmul(ps_t[:, bv, (J % 4) * 128:(J % 4) * 128 + 128], lhsT=la,
                                                                                       rhs=wv[g].ap[:, k * 128:(k + 1) * 128], start=(k == 0), stop=(k == NCH - 1)),
                                 reads=[lk, wv[g].k()], writes=[psk(bv)])
                        if J % 4 == 3:
                            J0 = J - 3
                            va3 = VA[g].ap[:, J0 * 128:(J0 + 4) * 128]
                            P.op("act", lambda e, va3=va3, bv=bv: e.activation(out=va3, in_=ps_t[:, bv, :], func=AF.Copy), reads=[psk(bv)], writes=[VA[g].k(J0 * 128, (J0 + 4) * 128)])
                            P.ps_release(bv)

            def q_proj(unit):
                kp, j, g = unit
                w, d = groups[g]
                hA = 8 * kp + j
                hB = 8 * kp + 4 + j
                qb = QT[qcnt[0] % 2]
                wqb = wq[qcnt[0] % 2]
                qcnt[0] += 1
                wq3 = wqb.ap.rearrange("p (k n) -> p k n", k=NCH)
                wload(wq3[:, :, 0:64], wqb.k(), win3[:, :, g * 1536 + hA * 64:g * 1536 + hA * 64 + 64], k_win)
                wload(wq3[:, :, 64:128], wqb.k(), win3[:, :, g * 1536 + hB * 64:g * 1536 + hB * 64 + 64], k_win)
                proj_rope_many([dict(w=wqb, tb=tb, dst=qb, d=d) for tb in range(NTB)])
                return qb

            def attention_unit(unit, qb):
                kp, j, g = unit
                w, d = groups[g]
                Ls = S // d
                mask_t = m128_t if w == 128 else m64_t
                mmask_t = mm128_t if w == 128 else mm64_t
                tiles_per_sub = Ls // 128
                kt = KT[g]
                va = VA[g]
                rng = {}
                for J in range(16):
                    jl = J % tiles_per_sub
                    sub0 = (J // tiles_per_sub) * Ls
                    qlo = max(0, 128 * jl - w)
                    qhi = min(Ls, 128 * jl + 128 + w)
                    rng[J] = (sub0 + qlo, sub0 + qhi, qlo - (128 * jl - w))
                last_J = {}
                for J in range(16):
                    for m in range(rng[J][0] // 512, (rng[J][1] - 1) // 512 + 1):
                        last_J[m] = J
                acc_banks = {}
                started = set()

                def st_tile(J):
                    glo, ghi, moff = rng[J]
                    nq = ghi - glo
                    pts = PT[J % NPT]
                    banks = [P.ps_alloc(), P.ps_alloc()]
                    for hh in range(2):
                        if MASK_ON_PE[hh]:
                            b = banks[hh]
                            P.op("pe", lambda e, b=b, nq=nq, moff=moff: e.matmul(ps_t[:, b, 0:nq], lhsT=identb_t[:, :], rhs=mask_t[:, moff:moff + nq],
                                                                                start=True, stop=False, skip_group_check=True),
                                 reads=[KC["identb"], KC["m128"], KC["m64"]], writes=[psk(b)])
                    for hh in range(2):
                        rows = slice(hh * 64, hh * 64 + 64)
                        b = banks[hh]
                        P.op("pe", lambda e, b=b, rows=rows, J=J, nq=nq, glo=glo, hh=hh: e.matmul(
                            ps_t[:, b, 0:nq], lhsT=kt.ap[rows, 128 * J:128 * J + 128], rhs=qb.ap[rows, glo:glo + nq],
                            start=(not MASK_ON_PE[hh]), stop=True, skip_group_check=True),
                            reads=[kt.k(128 * J, 128 * J + 128), qb.k(glo, glo + nq)], writes=[psk(b)])
                    for hh in range(2):
                        b = banks[hh]
                        pt = pts[hh]
                        P.op("act", lambda e, b=b, pt=pt, nq=nq: e.activation(out=pt.ap[:, 0:nq], in_=ps_t[:, b, 0:nq], func=AF.Exp, scale=0.125),
                             reads=[psk(b)], writes=[pt.k(0, nq)])
                        P.ps_release(b)
                        if not MASK_ON_PE[hh]:
                            P.op("pool", lambda e, pt=pt, nq=nq, moff=moff: e.tensor_tensor(out=pt.ap[:, 0:nq], in0=pt.ap[:, 0:nq], in1=mmask_t[:, moff:moff + nq], op=ALU.mult),
                                 reads=[pt.k(0, nq), KC["mm"]], writes=[pt.k(0, nq)])

                def pv_tile(J):
                    glo, ghi, _ = rng[J]
                    for m in range(glo // 512, (ghi - 1) // 512 + 1):
                        lo = max(glo, 512 * m)
                        hi = min(ghi, 512 * (m + 1))
                        if m not in acc_banks:
                            acc_banks[m] = (P.ps_alloc(), P.ps_alloc())
                        for (bi, use_v) in ((0, True), (1, False)):
                            bO = acc_banks[m][bi]
                            for hh in range(2):
                                first = (m, bi, hh) not in started
                                started.add((m, bi, hh))
                                pt = PT[J % NPT][hh]
                                lhs = va.ap[:, J * 128 + hh * 64:J * 128 + hh * 64 + 64] if use_v else ones_t[:, 0:64]
                                lk = va.k(J * 128, (J + 1) * 128) if use_v else KC["ones"]
                                P.op("pe", lambda e, bO=bO, pt=pt, lo=lo, hi=hi, glo=glo, m=m, first=first, lhs=lhs, hh=hh: e.matmul(
                                    ps_t[hh * 64:hh * 64 + 64, bO, lo - 512 * m:hi - 512 * m], lhsT=lhs, rhs=pt.ap[:, lo - glo:hi - glo],
                                    start=first, stop=False, skip_group_check=True),
                                    reads=[lk, pt.k(lo - glo, hi - glo)], writes=[psk(bO)])
                        if last_J[m] == J:
                            bA, bB = acc_banks.pop(m)
                            p0 = 512 * m
                            for acc, bO in ((accA, bA), (accB, bB)):
                                if d == 1:
                                    da = acc.ap[:, p0:p0 + 512]
                                    src = ps_t[:, bO, :]
                                elif d == 4:
                                    r = p0 // Ls
                                    da = acc.ap[:, r:S:4]
                                    src = ps_t[:, bO, :]
                                else:
                                    r0 = p0 // Ls
                                    da = acc.ap.rearrange("p (u r) -> p r u", r=16)[:, r0:r0 + 4, :]
                                    src = ps_t[:, bO, :].rearrange("p (r u) -> p r u", r=4)
                                dk = acc.k(p0, p0 + 512) if d == 1 else acc.k()
                                if g == 0:
                                    P.op("act", lambda e, da=da, src=src: e.activation(out=da, in_=src, func=AF.Copy), reads=[psk(bO)], writes=[dk])
                                else:
                                    P.op("dve", lambda e, da=da, src=src: e.tensor_tensor(out=da, in0=da, in1=src, op=ALU.add), reads=[psk(bO), dk], writes=[dk])
                                P.ps_release(bO)

                LAG = 2
                for step in range(16 + LAG):
                    if step < 16:
                        st_tile(step)
                    J = step - LAG
                    if J >= 0:
                        pv_tile(J)

            def normalize(kp, j):
                c = kp * 4 + j
                pl = j % 2
                if cfg["sink"]:
                    P.op("dve", lambda e, c=c: e.tensor_scalar(out=accB.ap, in0=accB.ap, scalar1=es_t[:, c:c + 1], scalar2=None, op0=ALU.add),
                         reads=[accB.k(), KC["es"]], writes=[accB.k()])
                for tb in range(NTB):
                    t0 = tb * TB
                    P.op("dve", lambda e, t0=t0: e.reciprocal(out=accB.ap[:, t0:t0 + TB], in_=accB.ap[:, t0:t0 + TB]), reads=[accB.k(t0, t0 + TB)], writes=[accB.k(t0, t0 + TB)])
                    oa = OT.ap[:, pl * S + t0:pl * S + t0 + TB]
                    P.op("pool", lambda e, oa=oa, t0=t0: e.tensor_tensor(out=oa, in0=accA.ap[:, t0:t0 + TB], in1=accB.ap[:, t0:t0 + TB], op=ALU.mult),
                         reads=[accA.k(t0, t0 + TB), accB.k(t0, t0 + TB)], writes=[OT.k(pl * S + t0, pl * S + t0 + TB)])

            def out_proj(kp, jj):
                wo3 = wo.ap.rearrange("p (j n) -> p j n", j=2)
                for s_ in range(2):
                    j = 2 * jj + s_
                    hA = 8 * kp + j
                    hB = 8 * kp + 4 + j
                    wload(wo3[0:64, s_, :], wo.k(), wo_d[hA * 64:hA * 64 + 64, :], k_wout)
                    wload(wo3[64:128, s_, :], wo.k(), wo_d[hB * 64:hB * 64 + 64, :], k_wout)
                for dc in range(NCH):
                    for tb in range(NTB):
                        b = P.ps_alloc()
                        for j in range(2):
                            P.op("pe", lambda e, b=b, j=j, dc=dc, tb=tb: e.matmul(ps_t[:, b, :], lhsT=wo.ap[:, j * D + dc * 128:j * D + dc * 128 + 128],
                                                                                   rhs=OT.ap[:, j * S + tb * TB:j * S + (tb + 1) * TB], start=(j == 0), stop=(j == 1)),
                                 reads=[wo.k(), OT.k(j * S + tb * TB, j * S + (tb + 1) * TB)], writes=[psk(b)])
                        xa, xk = xs(dc, tb * TB, TB)
                        P.op("dve", lambda e, xa=xa, b=b: e.tensor_tensor(out=xa, in0=xa, in1=ps_t[:, b, :], op=ALU.add), reads=[xk, psk(b)], writes=[xk])
                        P.ps_release(b)

            kv_proj(0)
            for kp in range(2):
                if kp == 1 and layer == 0:
                    do_casts("attn0")
                units = [(kp, j, g) for j in range(4) for g in range(G)]
                qb_next = q_proj(units[0])
                for i, unit in enumerate(units):
                    qb_cur = qb_next
                    if i + 1 < len(units):
                        qb_next = q_proj(units[i + 1])
                    attention_unit(unit, qb_cur)
                    if unit[2] == G - 1:
                        j = unit[1]
                        normalize(kp, j)
                        if j % 2 == 1:
                            if j == 3 and kp + 1 < 2:
                                kv_proj(kp + 1)
                            out_proj(kp, j // 2)
            A.pop()

        def ffn_layer(layer, after_half=None):
            A.push()
            TH = 1024
            actT = A.alloc(NFC * TH, BF16)
            wgb = [A.alloc(NCH * 256, BF16) for _ in range(2)]
            wub = [A.alloc(NCH * 256, BF16) for _ in range(2)]
            wdb = [A.alloc(NFC * 128, BF16) for _ in range(2)]
            sg = [A.alloc(TB, F32) for _ in range(2)]
            cnt = 0
            if layer == 0:
                do_casts("ffn0")
            for half in range(S // TH):
                for fp in range(NFC // 2):
                    gb, ub = wgb[fp % 2], wub[fp % 2]
                    wload(gb.ap.rearrange("p (k n) -> p k n", k=NCH), gb.k(), wg_b[layer].rearrange("(k p) n -> p k n", p=128)[:, :, fp * 256:(fp + 1) * 256], SK["wg%d" % layer])
                    wload(ub.ap.rearrange("p (k n) -> p k n", k=NCH), ub.k(), wu_b[layer].rearrange("(k p) n -> p k n", p=128)[:, :, fp * 256:(fp + 1) * 256], SK["wu%d" % layer])
                    for fc in range(2):
                        f = fp * 2 + fc
                        for tbl in range(TH // TB):
                            tb = half * (TH // TB) + tbl
                            bg = P.ps_alloc()
                            bu = P.ps_alloc()
                            for (wb, b) in ((gb, bg), (ub, bu)):
                                for k in range(NCH):
                                    ha, hk = hs(k, tb * TB, TB)
                                    P.op("pe", lambda e, wb=wb, b=b, k=k, ha=ha, fc=fc: e.matmul(ps_t[:, b, :], lhsT=wb.ap[:, k * 256 + fc * 128:k * 256 + fc * 128 + 128], rhs=ha,
                                                                                                 start=(k == 0), stop=(k == NCH - 1)),
                                         reads=[wb.k(), hk], writes=[psk(b)])
                            s_ = sg[cnt % 2]
                            cnt += 1
                            P.op("act", lambda e, s_=s_, bg=bg: e.activation(out=s_.ap, in_=ps_t[:, bg, :], func=AF.Silu), reads=[psk(bg)], writes=[s_.k()])
                            P.ps_release(bg)
                            aa = actT.ap[:, f * TH + tbl * TB:f * TH + (tbl + 1) * TB]
                            ak = actT.k(f * TH + tbl * TB, f * TH + (tbl + 1) * TB)
                            P.op("dve", lambda e, aa=aa, s_=s_, bu=bu: e.tensor_tensor(out=aa, in0=s_.ap, in1=ps_t[:, bu, :], op=ALU.mult), reads=[s_.k(), psk(bu)], writes=[ak])
                            P.ps_release(bu)
                for dc in range(NCH):
                    db = wdb[dc % 2]
                    wload(db.ap.rearrange("p (k n) -> p k n", k=NFC), db.k(), wd_b[layer].rearrange("(k p) n -> p k n", p=128)[:, :, dc * 128:(dc + 1) * 128], SK["wd%d" % layer])
                    for tbl in range(TH // TB):
                        tb = half * (TH // TB) + tbl
                        b = P.ps_alloc()
                        for f in range(NFC):
                            P.op("pe", lambda e, b=b, f=f, db=db, tbl=tbl: e.matmul(ps_t[:, b, :], lhsT=db.ap[:, f * 128:(f + 1) * 128],
                                                                                     rhs=actT.ap[:, f * TH + tbl * TB:f * TH + (tbl + 1) * TB], start=(f == 0), stop=(f == NFC - 1)),
                                 reads=[db.k(), actT.k(f * TH + tbl * TB, f * TH + (tbl + 1) * TB)], writes=[psk(b)])
                        xa, xk = xs(dc, tb * TB, TB)
                        P.op("dve", lambda e, xa=xa, b=b: e.tensor_tensor(out=xa, in0=xa, in1=ps_t[:, b, :], op=ALU.add), reads=[xk, psk(b)], writes=[xk])
                        P.ps_release(b)
                if after_half is not None:
                    after_half(half)
            A.pop()

        finals = []
        layers = list(layer_list if layer_list is not None else range(nlayers))
        for seq in range(nseq):
            load_x(seq)
            do_casts("after_load")
            A.push()
            rstd_f = None
            pre_normed = False
            for li, layer in enumerate(layers):
                if do_attn:
                    if not pre_normed:
                        rmsnorm_to_hT(layer)
                    attention_layer(layer)
                pre_normed = False
                if do_ffn:
                    rmsnorm_to_hT(2 + layer)
                    cb = None
                    if li + 1 < len(layers) and do_attn:
                        nxt = layers[li + 1]
                        cb = lambda half, nxt=nxt: rmsnorm_to_hT(nxt, [2 * half, 2 * half + 1])
                        pre_normed = True
                    elif li + 1 == len(layers) and final_norm:
                        rstd_f = A.alloc(S, F32)
                        sq_f = [A.alloc(TB, BF16) for _ in range(2)]
                        cb = lambda half: norm_stats(rstd_f, [2 * half, 2 * half + 1], sq_f)
                    ffn_layer(layer, cb)
            finals += store_out(seq, 4 if final_norm else None, rstd_f)
            A.pop()
        counts = P.build(final_wait_ops=finals)
        counts['arena_hw'] = getattr(A, 'hw', 0)
        counts['arena'] = ARENA_BYTES
    return nc, counts, len(P.ops)


def _consts():
    inv_freq = (1.0 / (np.float32(10000.0) ** (np.arange(0, 64, 2, dtype=np.float32) / np.float32(64)))).astype(np.float32)
    ang = (np.arange(S, dtype=np.float32)[:, None] * inv_freq[None, :]).astype(np.float32)
    cos = np.cos(ang).astype(np.float32).T
    sin = np.sin(ang).astype(np.float32).T
    ropec = np.tile(cos, (4, 1)).astype(np.float32)
    sgn = np.where((np.arange(128) % 64) < 32, -1.0, 1.0).astype(np.float32)[:, None]
    ropes = (np.tile(sin, (4, 1)) * sgn).astype(np.float32)
    ident = np.eye(128, dtype=np.float32)
    rperm = np.zeros((128, 128), np.float32)
    for m in range(128):
        partner = m + 32 if (m % 64) < 32 else m - 32
        rperm[partner, m] = 1.0
    b = np.arange(128)[:, None]
    a128 = np.arange(384)[None, :]
    m128 = (((b <= a128) & (a128 <= b + 256)).astype(np.float32) - 1.0) * 30000.0
    a64 = np.arange(256)[None, :]
    m64 = (((b <= a64) & (a64 <= b + 128)).astype(np.float32) - 1.0) * 30000.0
    mm128 = ((b <= a128) & (a128 <= b + 256)).astype(np.float32)
    mm64 = ((b <= a64) & (a64 <= b + 128)).astype(np.float32)
    return dict(ropec=ropec, ropes=ropes, ident=ident, rperm=rperm, m128=m128, m64=m64, mm128=mm128, mm64=mm64)


_CACHE = {}


def _prep_shared(a_w_in, a_sink, a_w_out, b_w_in, b_w_out, norm_mix, norm_ffn, w_gate, w_up, w_down, final_norm):
    f = lambda v: np.ascontiguousarray(np.asarray(v, dtype=np.float32))
    gains_src = [f(norm_mix)[0], f(norm_mix)[1], f(norm_ffn)[0], f(norm_ffn)[1], f(final_norm)]
    gains = np.concatenate([g.reshape(8, 128).T for g in gains_src], axis=1)
    sk = f(a_sink)[0]
    colsA = [8 * kp + j for kp in range(2) for j in range(4)]
    colsB = [8 * kp + 4 + j for kp in range(2) for j in range(4)]
    sinkb = np.concatenate([np.broadcast_to(sk[colsA][None, :], (64, 8)), np.broadcast_to(sk[colsB][None, :], (64, 8))], axis=0)
    d = dict(a_w_in=f(a_w_in)[0], a_w_out=f(a_w_out)[0], b_w_in=f(b_w_in)[0], b_w_out=f(b_w_out)[0],
             w_gate=f(w_gate), w_up=f(w_up), w_down=f(w_down),
             gains=np.ascontiguousarray(gains), sinkb=np.ascontiguousarray(sinkb))
    d.update(_consts())
    return d


def kernel(x, a_w_in, a_sink, a_w_out, b_w_in, b_w_out, norm_mix, norm_ffn, w_gate, w_up, w_down, final_norm):
    x = np.asarray(x, dtype=np.float32)
    B = x.shape[0]
    per = B // NCORES
    if "nc" not in _CACHE:
        _CACHE["nc"] = build_program(per, 2, True)[0]
    nc = _CACHE["nc"]
    shared = _prep_shared(a_w_in, a_sink, a_w_out, b_w_in, b_w_out, norm_mix, norm_ffn, w_gate, w_up, w_down, final_norm)
    in_maps = []
    for i in range(NCORES):
        m = dict(shared)
        m["x"] = np.ascontiguousarray(x[i * per:(i + 1) * per].reshape(per * S, D))
        in_maps.append(m)
    res = run_bass_kernel_spmd(nc, in_maps, core_ids=list(range(NCORES)))
    out = np.concatenate([np.asarray(r["out"], dtype=np.float32).reshape(per, S, D) for r in res.results], axis=0)
    return out
```

```python
import contextlib
import numpy as np
import concourse.bass as bass
import concourse.mybir as mybir
from concourse.bass_utils import run_bass_kernel_spmd

F32 = mybir.dt.float32
BF16 = mybir.dt.bfloat16
AF = mybir.ActivationFunctionType
ALU = mybir.AluOpType

S = 2048
D = 1024
DFF = 2816
NCH = 8
NFC = 22
TB = 512
NTB = 4
NCORES = 8
SEQ_PER_CORE = 4
ENGS = ("pe", "act", "dve", "pool", "sp")
SAME_ENGINE_SYNC = True


class Prog:
    def __init__(self, nc):
        self.nc = nc
        self.ops = []
        self.iv = {}
        self.dma_slots = {}
        self.ps_free = list(range(8))

    def _access(self, idx, eng_id, key, is_write, deps):
        space, lo, hi = key
        lst = self.iv.setdefault(space, [])
        new = []
        for rec in lst:
            l, h, i, w, e = rec
            if l < hi and lo < h:
                if is_write or w:
                    deps.add(i)
                if is_write and lo <= l and h <= hi:
                    continue
                if (not is_write) and (not w) and e == eng_id and l == lo and h == hi:
                    continue
            new.append(rec)
        new.append((lo, hi, idx, is_write, eng_id))
        self.iv[space] = new

    def _record(self, eng, emit, reads, writes, dma):
        idx = len(self.ops)
        deps = set()
        eng_id = eng if dma is None else ("dma", idx)
        for k in reads:
            self._access(idx, eng_id, k, False, deps)
        for k in writes:
            self._access(idx, eng_id, k, True, deps)
        deps.discard(idx)
        self.ops.append(dict(eng=eng, emit=emit, deps=deps, dma=dma, signal=False))
        return idx

    def op(self, eng, emit, reads=(), writes=()):
        return self._record(eng, emit, reads, writes, None)

    def dma(self, queue, emit, slot, reads=(), writes=()):
        cnt = self.dma_slots.get(slot, 0) + 16
        self.dma_slots[slot] = cnt
        return self._record(queue, emit, reads, writes, (slot, cnt))

    def ps_alloc(self):
        assert self.ps_free, "out of PSUM banks"
        return self.ps_free.pop(0)

    def ps_release(self, b):
        self.ps_free.append(b)

    def build(self, final_wait_ops=()):
        nc = self.nc
        ops = self.ops
        def needs_edge(x, o):
            if x["dma"] is not None:
                return True
            if o["dma"] is not None:
                return True
            if x["eng"] != o["eng"]:
                return True
            return SAME_ENGINE_SYNC and x["eng"] != "pe"

        for o in ops:
            for d in o["deps"]:
                x = ops[d]
                if x["dma"] is None and needs_edge(x, o):
                    x["signal"] = True
        counts = {e: 0 for e in ENGS}
        for o in ops:
            if o["dma"] is None and o["signal"]:
                counts[o["eng"]] += 1
                o["cnt"] = counts[o["eng"]]
        with contextlib.ExitStack() as st:
            esem = {e: st.enter_context(nc.semaphore("s_" + e)) for e in ENGS}
            dsem = {s: st.enter_context(nc.semaphore("d_%s" % (s,))) for s in self.dma_slots}
            block = st.enter_context(nc.Block())
            per_eng = {e: [] for e in ENGS}
            for i, o in enumerate(ops):
                per_eng[o["eng"]].append(i)

            def run(eng_name, eng):
                waited = {}
                for i in per_eng[eng_name]:
                    o = ops[i]
                    need = {}
                    for d in o["deps"]:
                        x = ops[d]
                        if x["dma"] is not None:
                            key = ("d", x["dma"][0])
                            val = x["dma"][1]
                        else:
                            if not needs_edge(x, o):
                                continue
                            key = ("e", x["eng"])
                            val = x["cnt"]
                        if val > need.get(key, 0):
                            need[key] = val
                    for key, val in need.items():
                        if waited.get(key, 0) >= val:
                            continue
                        waited[key] = val
                        sem = dsem[key[1]] if key[0] == "d" else esem[key[1]]
                        eng.wait_ge(sem, val)
                    ins = o["emit"](eng)
                    if o["dma"] is not None:
                        ins.then_inc(dsem[o["dma"][0]], 16)
                    elif o["signal"]:
                        ins.then_inc(esem[eng_name], 1)
                if eng_name == "sp":
                    for i in final_wait_ops:
                        x = ops[i]
                        eng.wait_ge(dsem[x["dma"][0]], x["dma"][1])

            @block.tensor
            def _(e):
                run("pe", e)

            @block.scalar
            def _(e):
                run("act", e)

            @block.vector
            def _(e):
                run("dve", e)

            @block.gpsimd
            def _(e):
                run("pool", e)

            @block.sync
            def _(e):
                run("sp", e)
        return counts


class Buf:
    def __init__(self, space, ap2d, byte_off, dt):
        self.space = space
        self.ap = ap2d
        self.off = byte_off
        self.esz = 2 if dt == BF16 else 4

    def k(self, lo=0, hi=None):
        if hi is None:
            hi = self.ap.shape[1]
        return (self.space, self.off + lo * self.esz, self.off + hi * self.esz)


class Arena:
    def __init__(self, t, nbytes):
        self.t = t
        self.nbytes = nbytes
        self.cur = 0
        self.marks = []

    def alloc(self, nelem, dt):
        esz = 2 if dt == BF16 else 4
        nb = (nelem * esz + 63) // 64 * 64
        off = self.cur
        assert off + nb <= self.nbytes, ("arena overflow", off + nb, self.nbytes)
        self.cur += nb
        self.hw = max(getattr(self, 'hw', 0), self.cur)
        v = self.t[:, off // 4:(off + nb) // 4]
        if dt == BF16:
            v = v.bitcast(BF16)
        v = v[:, 0:nelem]
        return Buf("arena", v, off, dt)

    def push(self):
        self.marks.append(self.cur)

    def pop(self):
        self.cur = self.marks.pop()


LAYER_CFG = [
    dict(groups=[(128, 1)], sink=True),
    dict(groups=[(64, 1), (64, 4), (64, 16)], sink=False),
]


def build_program(nseq=SEQ_PER_CORE, nlayers=2, final_norm=True, do_attn=True, do_ffn=True, layer_list=None, dbg_groups=None):
    nc = bass.Bass("TRN2", target_bir_lowering=False)
    dt_in = lambda name, shape: nc.dram_tensor(name, shape, F32, kind="ExternalInput").ap()
    x_d = dt_in("x", [nseq * S, D])
    win_d = [dt_in("a_w_in", [D, 1536]), dt_in("b_w_in", [D, 4608])]
    wout_d = [dt_in("a_w_out", [D, D]), dt_in("b_w_out", [D, D])]
    wg_d = dt_in("w_gate", [2, D, DFF])
    wu_d = dt_in("w_up", [2, D, DFF])
    wd_d = dt_in("w_down", [2, DFF, D])
    gains_d = dt_in("gains", [128, 40])
    sinkb_d = dt_in("sinkb", [128, 8])
    ropec_d = dt_in("ropec", [128, S])
    ropes_d = dt_in("ropes", [128, S])
    ident_d = dt_in("ident", [128, 128])
    rperm_d = dt_in("rperm", [128, 128])
    m128_d = dt_in("m128", [128, 384])
    m64_d = dt_in("m64", [128, 256])
    out_d = nc.dram_tensor("out", [nseq * S, D], F32, kind="ExternalOutput").ap()

    st = contextlib.ExitStack()
    with st:
        def sbt(name, shape, dt):
            return st.enter_context(nc.sbuf_tensor("sb_" + name, shape, dt))

        xT_t = sbt("xT", [128, NCH * S], F32)
        hT_t = sbt("hT", [128, NCH * S], BF16)
        ropec_t = sbt("ropec", [128, S], F32)
        ropes_t = sbt("ropes", [128, S], F32)
        ident_t = sbt("identf", [128, 128], F32)
        identb_t = sbt("identb", [128, 128], BF16)
        rperm_t = sbt("rperm", [128, 128], BF16)
        ones_t = sbt("ones", [128, 128], BF16)
        m128_t = sbt("m128", [128, 384], BF16)
        m64_t = sbt("m64", [128, 256], BF16)
        gains_t = sbt("gains", [128, 40], F32)
        es_t = sbt("es", [128, 8], F32)
        ARENA_BYTES = 95232
        arena_t = sbt("arena", [128, ARENA_BYTES // 4], F32)
        ps_t = st.enter_context(nc.psum_tensor("ps", [128, 8, 512], F32))

        P = Prog(nc)
        A = Arena(arena_t, ARENA_BYTES)
        xT = Buf("xT", xT_t[:, :], 0, F32)
        hT = Buf("hT", hT_t[:, :], 0, BF16)
        CONST = ("const", 0, 1)
        CONSTS = None
        KC = {n: ("c_" + n, 0, 1) for n in ("ropec", "ropes", "ident", "gains", "es", "rperm", "m128", "m64", "ones", "identb")}

        def psk(b):
            return ("ps", b, b + 1)

        def xs(c, t0, n):
            return xT_t[:, c * S + t0:c * S + t0 + n], xT.k(c * S + t0, c * S + t0 + n)

        def hs(c, t0, n, step=1):
            end = t0 + (n - 1) * step + 1
            return hT_t[:, c * S + t0:c * S + end:step], hT.k(c * S + t0, c * S + end)

        P.dma("sp", lambda e: e.dma_start(out=ropec_t[:, :], in_=ropec_d), "c0", writes=[KC["ropec"]])
        P.dma("sp", lambda e: e.dma_start(out=ropes_t[:, :], in_=ropes_d), "c1", writes=[KC["ropes"]])
        P.dma("sp", lambda e: e.dma_start(out=ident_t[:, :], in_=ident_d), "c2", writes=[KC["ident"]])
        P.dma("sp", lambda e: e.dma_start(out=gains_t[:, :], in_=gains_d), "c3", writes=[KC["gains"]])
        P.dma("sp", lambda e: e.dma_start(out=es_t[:, :], in_=sinkb_d), "c4", writes=[KC["es"]])
        P.dma("pool", lambda e: e.dma_start(out=rperm_t[:, :], in_=rperm_d), "c5", writes=[KC["rperm"]])
        P.dma("pool", lambda e: e.dma_start(out=m128_t[:, :], in_=m128_d), "c6", writes=[KC["m128"]])
        P.dma("pool", lambda e: e.dma_start(out=m64_t[:, :], in_=m64_d), "c7", writes=[KC["m64"]])
        P.dma("pool", lambda e: e.dma_start(out=identb_t[:, :], in_=ident_d), "c8", writes=[KC["identb"]])
        P.op("dve", lambda e: e.memset(ones_t[:, :], 1.0), writes=[KC["ones"]])
        P.op("act", lambda e: e.activation(out=es_t[:, :], in_=es_t[:, :], func=AF.Exp), reads=[KC["es"]], writes=[KC["es"]])

        wslot = [0]
        def scratch(name, shape):
            return nc.dram_tensor("sc_" + name, shape, BF16).ap()
        win_b = [scratch("a_w_in", [D, 1536]), scratch("b_w_in", [D, 4608])]
        wout_b = [scratch("a_w_out", [D, D]), scratch("b_w_out", [D, D])]
        wg_b = scratch("w_gate", [2, D, DFF])
        wu_b = scratch("w_up", [2, D, DFF])
        wd_b = scratch("w_down", [2, DFF, D])
        SK = {}

        def cast1(name, dst, src):
            rows = dst.shape[0]
            step = rows // 8
            for i in range(8):
                P.dma("pool", lambda e, i=i: e.dma_start(out=dst[i * step:(i + 1) * step, :], in_=src[i * step:(i + 1) * step, :]),
                      "cast_" + name, writes=[("sc_" + name, i, i + 1)])

        cast_plan = {
            "start": [("win0", win_b[0], win_d[0]), ("wout0", wout_b[0], wout_d[0])],
            "after_load": [("wg0", wg_b[0], wg_d[0]), ("wu0", wu_b[0], wu_d[0]), ("wd0", wd_b[0], wd_d[0])],
            "attn0": [("win1", win_b[1], win_d[1]), ("wout1", wout_b[1], wout_d[1])],
            "ffn0": [("wg1", wg_b[1], wg_d[1]), ("wu1", wu_b[1], wu_d[1]), ("wd1", wd_b[1], wd_d[1])],
        }
        for lst in cast_plan.values():
            for (name, _d, _s) in lst:
                SK[name] = ("sc_" + name, 0, 8)

        def do_casts(tag):
            for (name, dst, src) in cast_plan.pop(tag, []):
                cast1(name, dst, src)

        do_casts("start")

        def wload(dst_ap, dst_key, src_ap, src_key):
            wslot[0] += 1
            P.dma("sp", lambda e: e.dma_start(out=dst_ap, in_=src_ap), "w%d" % (wslot[0] % 40), reads=[src_key], writes=[dst_key])

        def norm_stats(rstd, tbs, sq):
            for tb in tbs:
                b = P.ps_alloc()
                for c in range(NCH):
                    xa, xk = xs(c, tb * TB, TB)
                    sb_ = sq[(tb * NCH + c) % len(sq)]
                    P.op("act", lambda e, sb_=sb_, xa=xa: e.activation(out=sb_.ap, in_=xa, func=AF.Square),
                         reads=[xk], writes=[sb_.k()])
                    P.op("pe", lambda e, sb_=sb_, b=b, c=c: e.matmul(ps_t[:, b, :], lhsT=ones_t[:, :], rhs=sb_.ap,
                                                                      start=(c == 0), stop=(c == NCH - 1)),
                         reads=[sb_.k(), KC["ones"]], writes=[psk(b)])
                ra = rstd.ap[:, tb * TB:(tb + 1) * TB]
                rk = rstd.k(tb * TB, (tb + 1) * TB)
                P.op("act", lambda e, ra=ra, b=b: e.activation(out=ra, in_=ps_t[:, b, :], func=AF.Sqrt, bias=1e-6, scale=1.0 / D),
                     reads=[psk(b)], writes=[rk])
                P.ps_release(b)
                P.op("dve", lambda e, ra=ra: e.reciprocal(out=ra, in_=ra), reads=[rk], writes=[rk])

        def rmsnorm_to_hT(gain_idx, tbs=None):
            tbs = list(range(NTB)) if tbs is None else list(tbs)
            A.push()
            rstd = A.alloc(S, F32)
            sq = [A.alloc(TB, BF16) for _ in range(3)]
            ntmp = [A.alloc(TB, F32) for _ in range(2)]
            norm_stats(rstd, tbs, sq)
            for tb in tbs:
                ra = rstd.ap[:, tb * TB:(tb + 1) * TB]
                rk = rstd.k(tb * TB, (tb + 1) * TB)
                for c in range(NCH):
                    xa, xk = xs(c, tb * TB, TB)
                    ha, hk = hs(c, tb * TB, TB)
                    if c % 2 == 0:
                        P.op("dve", lambda e, ha=ha, xa=xa, ra=ra, c=c: e.scalar_tensor_tensor(
                            out=ha, in0=xa, scalar=gains_t[:, gain_idx * 8 + c:gain_idx * 8 + c + 1], in1=ra,
                            op0=ALU.mult, op1=ALU.mult), reads=[xk, rk, KC["gains"]], writes=[hk])
                    else:
                        tm = ntmp[(c // 2) % 2]
                        P.op("pool", lambda e, tm=tm, xa=xa, ra=ra: e.tensor_tensor(out=tm.ap, in0=xa, in1=ra, op=ALU.mult), reads=[xk, rk], writes=[tm.k()])
                        P.op("act", lambda e, tm=tm, ha=ha, c=c: e.activation(out=ha, in_=tm.ap, func=AF.Copy, scale=gains_t[:, gain_idx * 8 + c:gain_idx * 8 + c + 1]),
                             reads=[tm.k(), KC["gains"]], writes=[hk])
            A.pop()

        def load_x(seq):
            A.push()
            A.alloc(12288, F32)
            xin = [A.alloc(D, F32) for _ in range(4)]
            for tt in range(S // 128):
                xb = xin[tt % 4]
                r0 = seq * S + tt * 128
                P.dma("sp", lambda e, xb=xb, r0=r0: e.dma_start(out=xb.ap, in_=x_d[r0:r0 + 128, :]), "xin%d" % (tt % 4), writes=[xb.k()])
                for half in range(2):
                    b = P.ps_alloc()
                    for j in range(4):
                        c = half * 4 + j
                        P.op("pe", lambda e, xb=xb, b=b, j=j, c=c: e.transpose(out=ps_t[:, b, j * 128:(j + 1) * 128],
                                                                                 in_=xb.ap[:, c * 128:(c + 1) * 128], identity=ident_t[:, :]),
                             reads=[xb.k(), KC["ident"]], writes=[psk(b)])
                    dst = xT_t[:, :].rearrange("p (c t) -> p c t", c=NCH)[:, half * 4:half * 4 + 4, tt * 128:(tt + 1) * 128]
                    src = ps_t[:, b, :].rearrange("p (c t) -> p c t", c=4)
                    keys = [xT.k(c * S + tt * 128, c * S + tt * 128 + 128) for c in range(half * 4, half * 4 + 4)]
                    eng = "dve" if half == 0 else "act"
                    if eng == "dve":
                        P.op("dve", lambda e, dst=dst, src=src: e.tensor_copy(out=dst, in_=src), reads=[psk(b)], writes=keys)
                    else:
                        P.op("act", lambda e, dst=dst, src=src: e.activation(out=dst, in_=src, func=AF.Copy), reads=[psk(b)], writes=keys)
                    P.ps_release(b)
            A.pop()

        def store_out(seq, gain_idx, rstd=None):
            A.push()
            ost = [A.alloc(D, F32) for _ in range(3)]
            ytmp = [A.alloc(128 * NCH, F32) for _ in range(3)]
            last = []
            if gain_idx is not None and rstd is None:
                rstd = A.alloc(S, F32)
                sq = [A.alloc(TB, BF16) for _ in range(3)]
                norm_stats(rstd, range(NTB), sq)
            for tt in range(S // 128):
                yt = ytmp[tt % 3]
                ob = ost[tt % 3]
                for c in range(NCH):
                    xa, xk = xs(c, tt * 128, 128)
                    ya = yt.ap[:, c * 128:(c + 1) * 128]
                    yk = yt.k(c * 128, (c + 1) * 128)
                    eng = "dve" if (c % 2 == 0 or gain_idx is not None) else "pool"
                    if gain_idx is not None:
                        ra = rstd.ap[:, tt * 128:(tt + 1) * 128]
                        rk = rstd.k(tt * 128, (tt + 1) * 128)
                        P.op(eng, lambda e, ya=ya, xa=xa, ra=ra, c=c: e.scalar_tensor_tensor(
                            out=ya, in0=xa, scalar=gains_t[:, gain_idx * 8 + c:gain_idx * 8 + c + 1], in1=ra, op0=ALU.mult, op1=ALU.mult),
                            reads=[xk, rk, KC["gains"]], writes=[yk])
                    else:
                        P.op(eng, lambda e, ya=ya, xa=xa: e.tensor_copy(out=ya, in_=xa), reads=[xk], writes=[yk])
                for half in range(2):
                    b = P.ps_alloc()
                    for j in range(4):
                        c = half * 4 + j
                        P.op("pe", lambda e, yt=yt, b=b, j=j, c=c: e.transpose(out=ps_t[:, b, j * 128:(j + 1) * 128], in_=yt.ap[:, c * 128:(c + 1) * 128], identity=ident_t[:, :]),
                             reads=[yt.k(c * 128, (c + 1) * 128), KC["ident"]], writes=[psk(b)])
                    oa = ob.ap[:, half * 512:(half + 1) * 512]
                    ok_ = ob.k(half * 512, (half + 1) * 512)
                    if half == 0:
                        P.op("dve", lambda e, oa=oa, b=b: e.tensor_copy(out=oa, in_=ps_t[:, b, :]), reads=[psk(b)], writes=[ok_])
                    else:
                        P.op("act", lambda e, oa=oa, b=b: e.activation(out=oa, in_=ps_t[:, b, :], func=AF.Copy), reads=[psk(b)], writes=[ok_])
                    P.ps_release(b)
                r0 = seq * S + tt * 128
                last.append(P.dma("pool", lambda e, ob=ob, r0=r0: e.dma_start(out=out_d[r0:r0 + 128, :], in_=ob.ap), "oout%d" % (tt % 3), reads=[ob.k()]))
            A.pop()
            return last[-3:]

        def attention_layer(layer):
            cfg = LAYER_CFG[layer]
            groups = cfg["groups"] if dbg_groups is None else dbg_groups
            G = len(groups)
            win = win_b[layer]
            wo_d = wout_b[layer]
            k_win = SK["win%d" % layer]
            k_wout = SK["wout%d" % layer]
            win3 = win.rearrange("(k p) n -> p k n", p=128)
            A.push()
            KT = [A.alloc(S, BF16) for _ in range(G)]
            VA = [A.alloc(16 * 128, BF16) for _ in range(G)]
            QT = [A.alloc(S, BF16) for _ in range(2)]
            OT = A.alloc(2 * S, BF16)
            accA = A.alloc(S, F32)
            accB = A.alloc(S, F32)
            NPT = 3
            ptw = max(128 + 2 * w_ for (w_, _d) in groups)
            PT = [[A.alloc(ptw, BF16) for _ in range(2)] for _ in range(NPT)]
            NRB = 3 if G == 1 else 2
            ropebufs = [(A.alloc(TB, BF16), A.alloc(TB, F32), A.alloc(TB, F32)) for _ in range(NRB)]
            wk = [A.alloc(NCH * 128, BF16) for _ in range(G)]
            wv = [A.alloc(NCH * 128, BF16) for _ in range(G)]
            wq = [A.alloc(NCH * 128, BF16) for _ in range(2)]
            wo = A.alloc(2 * D, BF16)
            ropecnt = [0]
            qcnt = [0]
            addcnt = [0]

            def rope_stage_a(it):
                wbuf, tb = it["w"], it["tb"]
                b = P.ps_alloc()
                for k in range(NCH):
                    ha, hk = hs(k, tb * TB, TB)
                    P.op("pe", lambda e, b=b, k=k, ha=ha, wbuf=wbuf: e.matmul(ps_t[:, b, :], lhsT=wbuf.ap[:, k * 128:(k + 1) * 128], rhs=ha,
                                                                               start=(k == 0), stop=(k == NCH - 1)),
                         reads=[wbuf.k(), hk], writes=[psk(b)])
                bufs = ropebufs[ropecnt[0] % NRB]
                ropecnt[0] += 1
                qraw = bufs[0]
                P.op("act", lambda e, b=b, qraw=qraw: e.activation(out=qraw.ap, in_=ps_t[:, b, :], func=AF.Copy), reads=[psk(b)], writes=[qraw.k()])
                P.ps_release(b)
                return bufs

            def rope_stage_b(it, bufs):
                qraw, tt_, uu = bufs
                tb, dst, d = it["tb"], it["dst"], it["d"]
                t0 = tb * TB
                b2 = P.ps_alloc()
                P.op("pe", lambda e: e.matmul(ps_t[:, b2, :], lhsT=rperm_t[:, :], rhs=qraw.ap, start=True, stop=True),
                     reads=[qraw.k(), KC["rperm"]], writes=[psk(b2)])
                P.op("dve", lambda e: e.tensor_tensor(out=tt_.ap, in0=ps_t[:, b2, :], in1=ropes_t[:, t0:t0 + TB], op=ALU.mult),
                     reads=[psk(b2), KC["ropes"]], writes=[tt_.k()])
                P.ps_release(b2)
                P.op("pool", lambda e: e.tensor_tensor(out=uu.ap, in0=qraw.ap, in1=ropec_t[:, t0:t0 + TB], op=ALU.mult),
                     reads=[qraw.k(), KC["ropec"]], writes=[uu.k()])
                if d == 1:
                    da = dst.ap[:, t0:t0 + TB]
                    dk = dst.k(t0, t0 + TB)
                    ia, ib = tt_.ap, uu.ap
                else:
                    da = dst.ap.rearrange("p (r u) -> p r u", r=d)[:, :, t0 // d:(t0 + TB) // d]
                    dk = dst.k()
                    ia = tt_.ap.rearrange("p (u r) -> p r u", r=d)
                    ib = uu.ap.rearrange("p (u r) -> p r u", r=d)
                addcnt[0] += 1
                P.op("dve" if addcnt[0] % 2 == 0 else "pool", lambda e: e.tensor_tensor(out=da, in0=ia, in1=ib, op=ALU.add), reads=[tt_.k(), uu.k()], writes=[dk])

            def proj_rope_many(items):
                pend = None
                for it in list(items) + [None]:
                    cur = None
                    if it is not None:
                        cur = (it, rope_stage_a(it))
                    if pend is not None:
                        rope_stage_b(*pend)
                    pend = cur

            def kv_proj(kp):
                items = []
                for g, (w, d) in enumerate(groups):
                    base = g * 1536
                    kcol = base + 1024 + kp * 128
                    vcol = base + 1280 + kp * 128
                    wload(wk[g].ap.rearrange("p (k n) -> p k n", k=NCH), wk[g].k(), win3[:, :, kcol:kcol + 128], k_win)
                    wload(wv[g].ap.rearrange("p (k n) -> p k n", k=NCH), wv[g].k(), win3[:, :, vcol:vcol + 128], k_win)
                    for tb in range(NTB):
                        items.append(dict(w=wk[g], tb=tb, dst=KT[g], d=d))
                proj_rope_many(items)
                for g, (w, d) in enumerate(groups):
                    Ls = S // d
                    for J in range(16):
                        r = (128 * J) // Ls
                        u0 = (128 * J) % Ls
                        if J % 4 == 0:
                            bv = P.ps_alloc()
                        for k in range(NCH):
                            la, lk = hs(k, r + d * u0, 128, step=d)
                            P.op("pe", lambda e, bv=bv, k=k, la=la, J=J, g=g: e.matmul(ps_t[:, bv, (J % 4) * 128:(J % 4) * 128 + 128], lhsT=la,
                                                                                       rhs=wv[g].ap[:, k * 128:(k + 1) * 128], start=(k == 0), stop=(k == NCH - 1)),
                                 reads=[lk, wv[g].k()], writes=[psk(bv)])
                        if J % 4 == 3:
                            J0 = J - 3
                            va3 = VA[g].ap[:, J0 * 128:(J0 + 4) * 128]
                            P.op("act", lambda e, va3=va3, bv=bv: e.activation(out=va3, in_=ps_t[:, bv, :], func=AF.Copy), reads=[psk(bv)], writes=[VA[g].k(J0 * 128, (J0 + 4) * 128)])
                            P.ps_release(bv)

            def q_proj(unit):
                kp, j, g = unit
                w, d = groups[g]
                hA = 8 * kp + j
                hB = 8 * kp + 4 + j
                qb = QT[qcnt[0] % 2]
                wqb = wq[qcnt[0] % 2]
                qcnt[0] += 1
                wq3 = wqb.ap.rearrange("p (k n) -> p k n", k=NCH)
                wload(wq3[:, :, 0:64], wqb.k(), win3[:, :, g * 1536 + hA * 64:g * 1536 + hA * 64 + 64], k_win)
                wload(wq3[:, :, 64:128], wqb.k(), win3[:, :, g * 1536 + hB * 64:g * 1536 + hB * 64 + 64], k_win)
                proj_rope_many([dict(w=wqb, tb=tb, dst=qb, d=d) for tb in range(NTB)])
                return qb

            def attention_unit(unit, qb):
                kp, j, g = unit
                w, d = groups[g]
                Ls = S // d
                mask_t = m128_t if w == 128 else m64_t
                tiles_per_sub = Ls // 128
                kt = KT[g]
                va = VA[g]
                rng = {}
                for J in range(16):
                    jl = J % tiles_per_sub
                    sub0 = (J // tiles_per_sub) * Ls
                    qlo = max(0, 128 * jl - w)
                    qhi = min(Ls, 128 * jl + 128 + w)
                    rng[J] = (sub0 + qlo, sub0 + qhi, qlo - (128 * jl - w))
                last_J = {}
                for J in range(16):
                    for m in range(rng[J][0] // 512, (rng[J][1] - 1) // 512 + 1):
                        last_J[m] = J
                acc_banks = {}
                started = set()

                def st_tile(J):
                    glo, ghi, moff = rng[J]
                    nq = ghi - glo
                    pts = PT[J % NPT]
                    banks = [P.ps_alloc(), P.ps_alloc()]
                    for hh in range(2):
                        b = banks[hh]
                        P.op("pe", lambda e, b=b, nq=nq, moff=moff: e.matmul(ps_t[:, b, 0:nq], lhsT=identb_t[:, :], rhs=mask_t[:, moff:moff + nq],
                                                                            start=True, stop=False, skip_group_check=True),
                             reads=[KC["identb"], KC["m128"], KC["m64"]], writes=[psk(b)])
                    for hh in range(2):
                        rows = slice(hh * 64, hh * 64 + 64)
                        b = banks[hh]
                        P.op("pe", lambda e, b=b, rows=rows, J=J, nq=nq, glo=glo: e.matmul(
                            ps_t[:, b, 0:nq], lhsT=kt.ap[rows, 128 * J:128 * J + 128], rhs=qb.ap[rows, glo:glo + nq],
                            start=False, stop=True, skip_group_check=True),
                            reads=[kt.k(128 * J, 128 * J + 128), qb.k(glo, glo + nq)], writes=[psk(b)])
                    for hh in range(2):
                        b = banks[hh]
                        pt = pts[hh]
                        P.op("act", lambda e, b=b, pt=pt, nq=nq: e.activation(out=pt.ap[:, 0:nq], in_=ps_t[:, b, 0:nq], func=AF.Exp, scale=0.125),
                             reads=[psk(b)], writes=[pt.k(0, nq)])
                        P.ps_release(b)

                def pv_tile(J):
                    glo, ghi, _ = rng[J]
                    for m in range(glo // 512, (ghi - 1) // 512 + 1):
                        lo = max(glo, 512 * m)
                        hi = min(ghi, 512 * (m + 1))
                        if m not in acc_banks:
                            acc_banks[m] = (P.ps_alloc(), P.ps_alloc())
                        for (bi, use_v) in ((0, True), (1, False)):
                            bO = acc_banks[m][bi]
                            for hh in range(2):
                                first = (m, bi, hh) not in started
                                started.add((m, bi, hh))
                                pt = PT[J % NPT][hh]
                                lhs = va.ap[:, J * 128 + hh * 64:J * 128 + hh * 64 + 64] if use_v else ones_t[:, 0:64]
                                lk = va.k(J * 128, (J + 1) * 128) if use_v else KC["ones"]
                                P.op("pe", lambda e, bO=bO, pt=pt, lo=lo, hi=hi, glo=glo, m=m, first=first, lhs=lhs, hh=hh: e.matmul(
                                    ps_t[hh * 64:hh * 64 + 64, bO, lo - 512 * m:hi - 512 * m], lhsT=lhs, rhs=pt.ap[:, lo - glo:hi - glo],
                                    start=first, stop=False, skip_group_check=True),
                                    reads=[lk, pt.k(lo - glo, hi - glo)], writes=[psk(bO)])
                        if last_J[m] == J:
                            bA, bB = acc_banks.pop(m)
                            p0 = 512 * m
                            for acc, bO in ((accA, bA), (accB, bB)):
                                if d == 1:
                                    da = acc.ap[:, p0:p0 + 512]
                                    src = ps_t[:, bO, :]
                                elif d == 4:
                                    r = p0 // Ls
                                    da = acc.ap[:, r:S:4]
                                    src = ps_t[:, bO, :]
                                else:
                                    r0 = p0 // Ls
                                    da = acc.ap.rearrange("p (u r) -> p r u", r=16)[:, r0:r0 + 4, :]
                                    src = ps_t[:, bO, :].rearrange("p (r u) -> p r u", r=4)
                                dk = acc.k(p0, p0 + 512) if d == 1 else acc.k()
                                if g == 0:
                                    P.op("act", lambda e, da=da, src=src: e.activation(out=da, in_=src, func=AF.Copy), reads=[psk(bO)], writes=[dk])
                                else:
                                    P.op("dve", lambda e, da=da, src=src: e.tensor_tensor(out=da, in0=da, in1=src, op=ALU.add), reads=[psk(bO), dk], writes=[dk])
                                P.ps_release(bO)

                LAG = 2
                for step in range(16 + LAG):
                    if step < 16:
                        st_tile(step)
                    J = step - LAG
                    if J >= 0:
                        pv_tile(J)

            def normalize(kp, j):
                c = kp * 4 + j
                pl = j % 2
                if cfg["sink"]:
                    P.op("dve", lambda e, c=c: e.tensor_scalar(out=accB.ap, in0=accB.ap, scalar1=es_t[:, c:c + 1], scalar2=None, op0=ALU.add),
                         reads=[accB.k(), KC["es"]], writes=[accB.k()])
                for tb in range(NTB):
                    t0 = tb * TB
                    P.op("dve", lambda e, t0=t0: e.reciprocal(out=accB.ap[:, t0:t0 + TB], in_=accB.ap[:, t0:t0 + TB]), reads=[accB.k(t0, t0 + TB)], writes=[accB.k(t0, t0 + TB)])
                    oa = OT.ap[:, pl * S + t0:pl * S + t0 + TB]
                    P.op("pool", lambda e, oa=oa, t0=t0: e.tensor_tensor(out=oa, in0=accA.ap[:, t0:t0 + TB], in1=accB.ap[:, t0:t0 + TB], op=ALU.mult),
                         reads=[accA.k(t0, t0 + TB), accB.k(t0, t0 + TB)], writes=[OT.k(pl * S + t0, pl * S + t0 + TB)])

            def out_proj(kp, jj):
                wo3 = wo.ap.rearrange("p (j n) -> p j n", j=2)
                for s_ in range(2):
                    j = 2 * jj + s_
                    hA = 8 * kp + j
                    hB = 8 * kp + 4 + j
                    wload(wo3[0:64, s_, :], wo.k(), wo_d[hA * 64:hA * 64 + 64, :], k_wout)
                    wload(wo3[64:128, s_, :], wo.k(), wo_d[hB * 64:hB * 64 + 64, :], k_wout)
                for dc in range(NCH):
                    for tb in range(NTB):
                        b = P.ps_alloc()
                        for j in range(2):
                            P.op("pe", lambda e, b=b, j=j, dc=dc, tb=tb: e.matmul(ps_t[:, b, :], lhsT=wo.ap[:, j * D + dc * 128:j * D + dc * 128 + 128],
                                                                                   rhs=OT.ap[:, j * S + tb * TB:j * S + (tb + 1) * TB], start=(j == 0), stop=(j == 1)),
                                 reads=[wo.k(), OT.k(j * S + tb * TB, j * S + (tb + 1) * TB)], writes=[psk(b)])
                        xa, xk = xs(dc, tb * TB, TB)
                        P.op("dve", lambda e, xa=xa, b=b: e.tensor_tensor(out=xa, in0=xa, in1=ps_t[:, b, :], op=ALU.add), reads=[xk, psk(b)], writes=[xk])
                        P.ps_release(b)

            kv_proj(0)
            for kp in range(2):
                if kp == 1 and layer == 0:
                    do_casts("attn0")
                units = [(kp, j, g) for j in range(4) for g in range(G)]
                qb_next = q_proj(units[0])
                for i, unit in enumerate(units):
                    qb_cur = qb_next
                    if i + 1 < len(units):
                        qb_next = q_proj(units[i + 1])
                    attention_unit(unit, qb_cur)
                    if unit[2] == G - 1:
                        j = unit[1]
                        normalize(kp, j)
                        if j % 2 == 1:
                            if j == 3 and kp + 1 < 2:
                                kv_proj(kp + 1)
                            out_proj(kp, j // 2)
            A.pop()

        def ffn_layer(layer, after_half=None):
            A.push()
            TH = 1024
            actT = A.alloc(NFC * TH, BF16)
            wgb = [A.alloc(NCH * 256, BF16) for _ in range(2)]
            wub = [A.alloc(NCH * 256, BF16) for _ in range(2)]
            wdb = [A.alloc(NFC * 128, BF16) for _ in range(2)]
            sg = [A.alloc(TB, F32) for _ in range(2)]
            cnt = 0
            if layer == 0:
                do_casts("ffn0")
            for half in range(S // TH):
                for fp in range(NFC // 2):
                    gb, ub = wgb[fp % 2], wub[fp % 2]
                    wload(gb.ap.rearrange("p (k n) -> p k n", k=NCH), gb.k(), wg_b[layer].rearrange("(k p) n -> p k n", p=128)[:, :, fp * 256:(fp + 1) * 256], SK["wg%d" % layer])
                    wload(ub.ap.rearrange("p (k n) -> p k n", k=NCH), ub.k(), wu_b[layer].rearrange("(k p) n -> p k n", p=128)[:, :, fp * 256:(fp + 1) * 256], SK["wu%d" % layer])
                    for fc in range(2):
                        f = fp * 2 + fc
                        for tbl in range(TH // TB):
                            tb = half * (TH // TB) + tbl
                            bg = P.ps_alloc()
                            bu = P.ps_alloc()
                            for (wb, b) in ((gb, bg), (ub, bu)):
                                for k in range(NCH):
                                    ha, hk = hs(k, tb * TB, TB)
                                    P.op("pe", lambda e, wb=wb, b=b, k=k, ha=ha, fc=fc: e.matmul(ps_t[:, b, :], lhsT=wb.ap[:, k * 256 + fc * 128:k * 256 + fc * 128 + 128], rhs=ha,
                                                                                                 start=(k == 0), stop=(k == NCH - 1)),
                                         reads=[wb.k(), hk], writes=[psk(b)])
                            s_ = sg[cnt % 2]
                            cnt += 1
                            P.op("act", lambda e, s_=s_, bg=bg: e.activation(out=s_.ap, in_=ps_t[:, bg, :], func=AF.Silu), reads=[psk(bg)], writes=[s_.k()])
                            P.ps_release(bg)
                            aa = actT.ap[:, f * TH + tbl * TB:f * TH + (tbl + 1) * TB]
                            ak = actT.k(f * TH + tbl * TB, f * TH + (tbl + 1) * TB)
                            P.op("dve", lambda e, aa=aa, s_=s_, bu=bu: e.tensor_tensor(out=aa, in0=s_.ap, in1=ps_t[:, bu, :], op=ALU.mult), reads=[s_.k(), psk(bu)], writes=[ak])
                            P.ps_release(bu)
                for dc in range(NCH):
                    db = wdb[dc % 2]
                    db3 = db.ap.rearrange("p (k n) -> p k n", k=NFC)
                    wd3 = wd_b[layer].rearrange("(k p) n -> p k n", p=128)[:, :, dc * 128:(dc + 1) * 128]
                    wload(db3[:, 0:11, :], db.k(0, 11 * 128), wd3[:, 0:11, :], SK["wd%d" % layer])
                    wload(db3[:, 11:22, :], db.k(11 * 128, 22 * 128), wd3[:, 11:22, :], SK["wd%d" % layer])
                    for tbl in range(TH // TB):
                        tb = half * (TH // TB) + tbl
                        b = P.ps_alloc()
                        for f in range(NFC):
                            P.op("pe", lambda e, b=b, f=f, db=db, tbl=tbl: e.matmul(ps_t[:, b, :], lhsT=db.ap[:, f * 128:(f + 1) * 128],
                                                                                     rhs=actT.ap[:, f * TH + tbl * TB:f * TH + (tbl + 1) * TB], start=(f == 0), stop=(f == NFC - 1)),
                                 reads=[db.k(), actT.k(f * TH + tbl * TB, f * TH + (tbl + 1) * TB)], writes=[psk(b)])
                        xa, xk = xs(dc, tb * TB, TB)
                        P.op("dve", lambda e, xa=xa, b=b: e.tensor_tensor(out=xa, in0=xa, in1=ps_t[:, b, :], op=ALU.add), reads=[xk, psk(b)], writes=[xk])
                        P.ps_release(b)
                if after_half is not None:
                    after_half(half)
            A.pop()

        finals = []
        layers = list(layer_list if layer_list is not None else range(nlayers))
        for seq in range(nseq):
            load_x(seq)
            do_casts("after_load")
            A.push()
            rstd_f = None
            pre_normed = False
            for li, layer in enumerate(layers):
                if do_attn:
                    if not pre_normed:
                        rmsnorm_to_hT(layer)
                    attention_layer(layer)
                pre_normed = False
                if do_ffn:
                    rmsnorm_to_hT(2 + layer)
                    cb = None
                    if li + 1 < len(layers) and do_attn:
                        nxt = layers[li + 1]
                        cb = lambda half, nxt=nxt: rmsnorm_to_hT(nxt, [2 * half, 2 * half + 1])
                        pre_normed = True
                    elif li + 1 == len(layers) and final_norm:
                        rstd_f = A.alloc(S, F32)
                        sq_f = [A.alloc(TB, BF16) for _ in range(2)]
                        cb = lambda half: norm_stats(rstd_f, [2 * half, 2 * half + 1], sq_f)
                    ffn_layer(layer, cb)
            finals += store_out(seq, 4 if final_norm else None, rstd_f)
            A.pop()
        counts = P.build(final_wait_ops=finals)
        counts['arena_hw'] = getattr(A, 'hw', 0)
        counts['arena'] = ARENA_BYTES
    return nc, counts, len(P.ops)


def _consts():
    inv_freq = (1.0 / (np.float32(10000.0) ** (np.arange(0, 64, 2, dtype=np.float32) / np.float32(64)))).astype(np.float32)
    ang = (np.arange(S, dtype=np.float32)[:, None] * inv_freq[None, :]).astype(np.float32)
    cos = np.cos(ang).astype(np.float32).T
    sin = np.sin(ang).astype(np.float32).T
    ropec = np.tile(cos, (4, 1)).astype(np.float32)
    sgn = np.where((np.arange(128) % 64) < 32, -1.0, 1.0).astype(np.float32)[:, None]
    ropes = (np.tile(sin, (4, 1)) * sgn).astype(np.float32)
    ident = np.eye(128, dtype=np.float32)
    rperm = np.zeros((128, 128), np.float32)
    for m in range(128):
        partner = m + 32 if (m % 64) < 32 else m - 32
        rperm[partner, m] = 1.0
    b = np.arange(128)[:, None]
    a128 = np.arange(384)[None, :]
    m128 = (((b <= a128) & (a128 <= b + 256)).astype(np.float32) - 1.0) * 30000.0
    a64 = np.arange(256)[None, :]
    m64 = (((b <= a64) & (a64 <= b + 128)).astype(np.float32) - 1.0) * 30000.0
    return dict(ropec=ropec, ropes=ropes, ident=ident, rperm=rperm, m128=m128, m64=m64)


_CACHE = {}


def _prep_shared(a_w_in, a_sink, a_w_out, b_w_in, b_w_out, norm_mix, norm_ffn, w_gate, w_up, w_down, final_norm):
    f = lambda v: np.ascontiguousarray(np.asarray(v, dtype=np.float32))
    gains_src = [f(norm_mix)[0], f(norm_mix)[1], f(norm_ffn)[0], f(norm_ffn)[1], f(final_norm)]
    gains = np.concatenate([g.reshape(8, 128).T for g in gains_src], axis=1)
    sk = f(a_sink)[0]
    colsA = [8 * kp + j for kp in range(2) for j in range(4)]
    colsB = [8 * kp + 4 + j for kp in range(2) for j in range(4)]
    sinkb = np.concatenate([np.broadcast_to(sk[colsA][None, :], (64, 8)), np.broadcast_to(sk[colsB][None, :], (64, 8))], axis=0)
    d = dict(a_w_in=f(a_w_in)[0], a_w_out=f(a_w_out)[0], b_w_in=f(b_w_in)[0], b_w_out=f(b_w_out)[0],
             w_gate=f(w_gate), w_up=f(w_up), w_down=f(w_down),
             gains=np.ascontiguousarray(gains), sinkb=np.ascontiguousarray(sinkb))
    d.update(_consts())
    return d


def kernel(x, a_w_in, a_sink, a_w_out, b_w_in, b_w_out, norm_mix, norm_ffn, w_gate, w_up, w_down, final_norm):
    x = np.asarray(x, dtype=np.float32)
    B = x.shape[0]
    per = B // NCORES
    if "nc" not in _CACHE:
        _CACHE["nc"] = build_program(per, 2, True)[0]
    nc = _CACHE["nc"]
    shared = _prep_shared(a_w_in, a_sink, a_w_out, b_w_in, b_w_out, norm_mix, norm_ffn, w_gate, w_up, w_down, final_norm)
    in_maps = []
    for i in range(NCORES):
        m = dict(shared)
        m["x"] = np.ascontiguousarray(x[i * per:(i + 1) * per].reshape(per * S, D))
        in_maps.append(m)
    res = run_bass_kernel_spmd(nc, in_maps, core_ids=list(range(NCORES)))
    out = np.concatenate([np.asarray(r["out"], dtype=np.float32).reshape(per, S, D) for r in res.results], axis=0)
    return out
```

```python
import contextlib
import numpy as np
import concourse.bass as bass
import concourse.mybir as mybir
from concourse.bass_utils import run_bass_kernel_spmd

F32 = mybir.dt.float32
BF16 = mybir.dt.bfloat16
AF = mybir.ActivationFunctionType
ALU = mybir.AluOpType

S = 2048
D = 1024
DFF = 2816
NCH = 8
NFC = 22
TB = 512
NTB = 4
NCORES = 8
SEQ_PER_CORE = 4
ENGS = ("pe", "act", "dve", "pool", "sp")
SAME_ENGINE_SYNC = True


class Prog:
    def __init__(self, nc):
        self.nc = nc
        self.ops = []
        self.iv = {}
        self.dma_slots = {}
        self.ps_free = list(range(8))

    def _access(self, idx, eng_id, key, is_write, deps):
        space, lo, hi = key
        lst = self.iv.setdefault(space, [])
        new = []
        for rec in lst:
            l, h, i, w, e = rec
            if l < hi and lo < h:
                if is_write or w:
                    deps.add(i)
                if is_write and lo <= l and h <= hi:
                    continue
                if (not is_write) and (not w) and e == eng_id and l == lo and h == hi:
                    continue
            new.append(rec)
        new.append((lo, hi, idx, is_write, eng_id))
        self.iv[space] = new

    def _record(self, eng, emit, reads, writes, dma):
        idx = len(self.ops)
        deps = set()
        eng_id = eng if dma is None else ("dma", idx)
        for k in reads:
            self._access(idx, eng_id, k, False, deps)
        for k in writes:
            self._access(idx, eng_id, k, True, deps)
        deps.discard(idx)
        self.ops.append(dict(eng=eng, emit=emit, deps=deps, dma=dma, signal=False))
        return idx

    def op(self, eng, emit, reads=(), writes=()):
        return self._record(eng, emit, reads, writes, None)

    def dma(self, queue, emit, slot, reads=(), writes=()):
        cnt = self.dma_slots.get(slot, 0) + 16
        self.dma_slots[slot] = cnt
        return self._record(queue, emit, reads, writes, (slot, cnt))

    def ps_alloc(self):
        assert self.ps_free, "out of PSUM banks"
        return self.ps_free.pop(0)

    def ps_release(self, b):
        self.ps_free.append(b)

    def build(self, final_wait_ops=()):
        nc = self.nc
        ops = self.ops
        def needs_edge(x, o):
            if x["dma"] is not None:
                return True
            if o["dma"] is not None:
                return True
            if x["eng"] != o["eng"]:
                return True
            return SAME_ENGINE_SYNC and x["eng"] != "pe"

        for o in ops:
            for d in o["deps"]:
                x = ops[d]
                if x["dma"] is None and needs_edge(x, o):
                    x["signal"] = True
        counts = {e: 0 for e in ENGS}
        for o in ops:
            if o["dma"] is None and o["signal"]:
                counts[o["eng"]] += 1
                o["cnt"] = counts[o["eng"]]
        with contextlib.ExitStack() as st:
            esem = {e: st.enter_context(nc.semaphore("s_" + e)) for e in ENGS}
            dsem = {s: st.enter_context(nc.semaphore("d_%s" % (s,))) for s in self.dma_slots}
            block = st.enter_context(nc.Block())
            per_eng = {e: [] for e in ENGS}
            for i, o in enumerate(ops):
                per_eng[o["eng"]].append(i)

            def run(eng_name, eng):
                waited = {}
                for i in per_eng[eng_name]:
                    o = ops[i]
                    need = {}
                    for d in o["deps"]:
                        x = ops[d]
                        if x["dma"] is not None:
                            key = ("d", x["dma"][0])
                            val = x["dma"][1]
                        else:
                            if not needs_edge(x, o):
                                continue
                            key = ("e", x["eng"])
                            val = x["cnt"]
                        if val > need.get(key, 0):
                            need[key] = val
                    for key, val in need.items():
                        if waited.get(key, 0) >= val:
                            continue
                        waited[key] = val
                        sem = dsem[key[1]] if key[0] == "d" else esem[key[1]]
                        eng.wait_ge(sem, val)
                    ins = o["emit"](eng)
                    if o["dma"] is not None:
                        ins.then_inc(dsem[o["dma"][0]], 16)
                    elif o["signal"]:
                        ins.then_inc(esem[eng_name], 1)
                if eng_name == "sp":
                    for i in final_wait_ops:
                        x = ops[i]
                        eng.wait_ge(dsem[x["dma"][0]], x["dma"][1])

            @block.tensor
            def _(e):
                run("pe", e)

            @block.scalar
            def _(e):
                run("act", e)

            @block.vector
            def _(e):
                run("dve", e)

            @block.gpsimd
            def _(e):
                run("pool", e)

            @block.sync
            def _(e):
                run("sp", e)
        return counts


class Buf:
    def __init__(self, space, ap2d, byte_off, dt):
        self.space = space
        self.ap = ap2d
        self.off = byte_off
        self.esz = 2 if dt == BF16 else 4

    def k(self, lo=0, hi=None):
        if hi is None:
            hi = self.ap.shape[1]
        return (self.space, self.off + lo * self.esz, self.off + hi * self.esz)


class Arena:
    def __init__(self, t, nbytes):
        self.t = t
        self.nbytes = nbytes
        self.cur = 0
        self.marks = []

    def alloc(self, nelem, dt):
        esz = 2 if dt == BF16 else 4
        nb = (nelem * esz + 63) // 64 * 64
        off = self.cur
        assert off + nb <= self.nbytes, ("arena overflow", off + nb, self.nbytes)
        self.cur += nb
        self.hw = max(getattr(self, 'hw', 0), self.cur)
        v = self.t[:, off // 4:(off + nb) // 4]
        if dt == BF16:
            v = v.bitcast(BF16)
        v = v[:, 0:nelem]
        return Buf("arena", v, off, dt)

    def push(self):
        self.marks.append(self.cur)

    def pop(self):
        self.cur = self.marks.pop()


LAYER_CFG = [
    dict(groups=[(128, 1)], sink=True),
    dict(groups=[(64, 1), (64, 4), (64, 16)], sink=False),
]


def build_program(nseq=SEQ_PER_CORE, nlayers=2, final_norm=True, do_attn=True, do_ffn=True, layer_list=None, dbg_groups=None):
    nc = bass.Bass("TRN2", target_bir_lowering=False)
    dt_in = lambda name, shape: nc.dram_tensor(name, shape, F32, kind="ExternalInput").ap()
    x_d = dt_in("x", [nseq * S, D])
    win_d = [dt_in("a_w_in", [D, 1536]), dt_in("b_w_in", [D, 4608])]
    wout_d = [dt_in("a_w_out", [D, D]), dt_in("b_w_out", [D, D])]
    wg_d = dt_in("w_gate", [2, D, DFF])
    wu_d = dt_in("w_up", [2, D, DFF])
    wd_d = dt_in("w_down", [2, DFF, D])
    gains_d = dt_in("gains", [128, 40])
    sinkb_d = dt_in("sinkb", [128, 8])
    ropec_d = dt_in("ropec", [128, S])
    ropes_d = dt_in("ropes", [128, S])
    ident_d = dt_in("ident", [128, 128])
    rperm_d = dt_in("rperm", [128, 128])
    m128_d = dt_in("m128", [128, 384])
    m64_d = dt_in("m64", [128, 256])
    out_d = nc.dram_tensor("out", [nseq * S, D], F32, kind="ExternalOutput").ap()

    st = contextlib.ExitStack()
    with st:
        def sbt(name, shape, dt):
            return st.enter_context(nc.sbuf_tensor("sb_" + name, shape, dt))

        xT_t = sbt("xT", [128, NCH * S], F32)
        hT_t = sbt("hT", [128, NCH * S], BF16)
        ropec_t = sbt("ropec", [128, S], F32)
        ropes_t = sbt("ropes", [128, S], F32)
        ident_t = sbt("identf", [128, 128], F32)
        identb_t = sbt("identb", [128, 128], BF16)
        rperm_t = sbt("rperm", [128, 128], BF16)
        ones_t = sbt("ones", [128, 128], BF16)
        m128_t = sbt("m128", [128, 384], BF16)
        m64_t = sbt("m64", [128, 256], BF16)
        gains_t = sbt("gains", [128, 40], F32)
        es_t = sbt("es", [128, 8], F32)
        ARENA_BYTES = 95232
        arena_t = sbt("arena", [128, ARENA_BYTES // 4], F32)
        ps_t = st.enter_context(nc.psum_tensor("ps", [128, 8, 512], F32))

        P = Prog(nc)
        A = Arena(arena_t, ARENA_BYTES)
        xT = Buf("xT", xT_t[:, :], 0, F32)
        hT = Buf("hT", hT_t[:, :], 0, BF16)
        CONST = ("const", 0, 1)
        CONSTS = None
        KC = {n: ("c_" + n, 0, 1) for n in ("ropec", "ropes", "ident", "gains", "es", "rperm", "m128", "m64", "ones", "identb")}

        def psk(b):
            return ("ps", b, b + 1)

        def xs(c, t0, n):
            return xT_t[:, c * S + t0:c * S + t0 + n], xT.k(c * S + t0, c * S + t0 + n)

        def hs(c, t0, n, step=1):
            end = t0 + (n - 1) * step + 1
            return hT_t[:, c * S + t0:c * S + end:step], hT.k(c * S + t0, c * S + end)

        P.dma("sp", lambda e: e.dma_start(out=ropec_t[:, :], in_=ropec_d), "c0", writes=[KC["ropec"]])
        P.dma("sp", lambda e: e.dma_start(out=ropes_t[:, :], in_=ropes_d), "c1", writes=[KC["ropes"]])
        P.dma("sp", lambda e: e.dma_start(out=ident_t[:, :], in_=ident_d), "c2", writes=[KC["ident"]])
        P.dma("sp", lambda e: e.dma_start(out=gains_t[:, :], in_=gains_d), "c3", writes=[KC["gains"]])
        P.dma("sp", lambda e: e.dma_start(out=es_t[:, :], in_=sinkb_d), "c4", writes=[KC["es"]])
        P.dma("pool", lambda e: e.dma_start(out=rperm_t[:, :], in_=rperm_d), "c5", writes=[KC["rperm"]])
        P.dma("pool", lambda e: e.dma_start(out=m128_t[:, :], in_=m128_d), "c6", writes=[KC["m128"]])
        P.dma("pool", lambda e: e.dma_start(out=m64_t[:, :], in_=m64_d), "c7", writes=[KC["m64"]])
        P.dma("pool", lambda e: e.dma_start(out=identb_t[:, :], in_=ident_d), "c8", writes=[KC["identb"]])
        P.op("dve", lambda e: e.memset(ones_t[:, :], 1.0), writes=[KC["ones"]])
        P.op("act", lambda e: e.activation(out=es_t[:, :], in_=es_t[:, :], func=AF.Exp), reads=[KC["es"]], writes=[KC["es"]])

        wslot = [0]
        def scratch(name, shape):
            return nc.dram_tensor("sc_" + name, shape, BF16).ap()
        win_b = [scratch("a_w_in", [D, 1536]), scratch("b_w_in", [D, 4608])]
        wout_b = [scratch("a_w_out", [D, D]), scratch("b_w_out", [D, D])]
        wg_b = scratch("w_gate", [2, D, DFF])
        wu_b = scratch("w_up", [2, D, DFF])
        wd_b = scratch("w_down", [2, DFF, D])
        SK = {}

        def cast1(name, dst, src):
            rows = dst.shape[0]
            step = rows // 8
            for i in range(8):
                P.dma("pool", lambda e, i=i: e.dma_start(out=dst[i * step:(i + 1) * step, :], in_=src[i * step:(i + 1) * step, :]),
                      "cast_" + name, writes=[("sc_" + name, i, i + 1)])

        cast_plan = {
            "start": [("win0", win_b[0], win_d[0]), ("wout0", wout_b[0], wout_d[0])],
            "after_load": [("wg0", wg_b[0], wg_d[0]), ("wu0", wu_b[0], wu_d[0]), ("wd0", wd_b[0], wd_d[0])],
            "attn0": [("win1", win_b[1], win_d[1]), ("wout1", wout_b[1], wout_d[1])],
            "ffn0": [("wg1", wg_b[1], wg_d[1]), ("wu1", wu_b[1], wu_d[1]), ("wd1", wd_b[1], wd_d[1])],
        }
        for lst in cast_plan.values():
            for (name, _d, _s) in lst:
                SK[name] = ("sc_" + name, 0, 8)

        def do_casts(tag):
            for (name, dst, src) in cast_plan.pop(tag, []):
                cast1(name, dst, src)

        do_casts("start")

        def wload(dst_ap, dst_key, src_ap, src_key):
            wslot[0] += 1
            P.dma("sp", lambda e: e.dma_start(out=dst_ap, in_=src_ap), "w%d" % (wslot[0] % 40), reads=[src_key], writes=[dst_key])

        def norm_stats(rstd, tbs, sq):
            for tb in tbs:
                b = P.ps_alloc()
                for c in range(NCH):
                    xa, xk = xs(c, tb * TB, TB)
                    sb_ = sq[(tb * NCH + c) % len(sq)]
                    P.op("act", lambda e, sb_=sb_, xa=xa: e.activation(out=sb_.ap, in_=xa, func=AF.Square),
                         reads=[xk], writes=[sb_.k()])
                    P.op("pe", lambda e, sb_=sb_, b=b, c=c: e.matmul(ps_t[:, b, :], lhsT=ones_t[:, :], rhs=sb_.ap,
                                                                      start=(c == 0), stop=(c == NCH - 1)),
                         reads=[sb_.k(), KC["ones"]], writes=[psk(b)])
                ra = rstd.ap[:, tb * TB:(tb + 1) * TB]
                rk = rstd.k(tb * TB, (tb + 1) * TB)
                P.op("act", lambda e, ra=ra, b=b: e.activation(out=ra, in_=ps_t[:, b, :], func=AF.Sqrt, bias=1e-6, scale=1.0 / D),
                     reads=[psk(b)], writes=[rk])
                P.ps_release(b)
                P.op("dve", lambda e, ra=ra: e.reciprocal(out=ra, in_=ra), reads=[rk], writes=[rk])

        def rmsnorm_to_hT(gain_idx, tbs=None):
            tbs = list(range(NTB)) if tbs is None else list(tbs)
            A.push()
            rstd = A.alloc(S, F32)
            sq = [A.alloc(TB, BF16) for _ in range(3)]
            ntmp = [A.alloc(TB, F32) for _ in range(2)]
            norm_stats(rstd, tbs, sq)
            for tb in tbs:
                ra = rstd.ap[:, tb * TB:(tb + 1) * TB]
                rk = rstd.k(tb * TB, (tb + 1) * TB)
                for c in range(NCH):
                    xa, xk = xs(c, tb * TB, TB)
                    ha, hk = hs(c, tb * TB, TB)
                    if c % 2 == 0:
                        P.op("dve", lambda e, ha=ha, xa=xa, ra=ra, c=c: e.scalar_tensor_tensor(
                            out=ha, in0=xa, scalar=gains_t[:, gain_idx * 8 + c:gain_idx * 8 + c + 1], in1=ra,
                            op0=ALU.mult, op1=ALU.mult), reads=[xk, rk, KC["gains"]], writes=[hk])
                    else:
                        tm = ntmp[(c // 2) % 2]
                        P.op("pool", lambda e, tm=tm, xa=xa, ra=ra: e.tensor_tensor(out=tm.ap, in0=xa, in1=ra, op=ALU.mult), reads=[xk, rk], writes=[tm.k()])
                        P.op("act", lambda e, tm=tm, ha=ha, c=c: e.activation(out=ha, in_=tm.ap, func=AF.Copy, scale=gains_t[:, gain_idx * 8 + c:gain_idx * 8 + c + 1]),
                             reads=[tm.k(), KC["gains"]], writes=[hk])
            A.pop()

        def load_x(seq):
            A.push()
            A.alloc(12288, F32)
            xin = [A.alloc(D, F32) for _ in range(4)]
            for tt in range(S // 128):
                xb = xin[tt % 4]
                r0 = seq * S + tt * 128
                P.dma("sp", lambda e, xb=xb, r0=r0: e.dma_start(out=xb.ap, in_=x_d[r0:r0 + 128, :]), "xin%d" % (tt % 4), writes=[xb.k()])
                for half in range(2):
                    b = P.ps_alloc()
                    for j in range(4):
                        c = half * 4 + j
                        P.op("pe", lambda e, xb=xb, b=b, j=j, c=c: e.transpose(out=ps_t[:, b, j * 128:(j + 1) * 128],
                                                                                 in_=xb.ap[:, c * 128:(c + 1) * 128], identity=ident_t[:, :]),
                             reads=[xb.k(), KC["ident"]], writes=[psk(b)])
                    dst = xT_t[:, :].rearrange("p (c t) -> p c t", c=NCH)[:, half * 4:half * 4 + 4, tt * 128:(tt + 1) * 128]
                    src = ps_t[:, b, :].rearrange("p (c t) -> p c t", c=4)
                    keys = [xT.k(c * S + tt * 128, c * S + tt * 128 + 128) for c in range(half * 4, half * 4 + 4)]
                    eng = "dve" if half == 0 else "act"
                    if eng == "dve":
                        P.op("dve", lambda e, dst=dst, src=src: e.tensor_copy(out=dst, in_=src), reads=[psk(b)], writes=keys)
                    else:
                        P.op("act", lambda e, dst=dst, src=src: e.activation(out=dst, in_=src, func=AF.Copy), reads=[psk(b)], writes=keys)
                    P.ps_release(b)
            A.pop()

        def store_out(seq, gain_idx, rstd=None):
            A.push()
            ost = [A.alloc(D, F32) for _ in range(3)]
            ytmp = [A.alloc(128 * NCH, F32) for _ in range(3)]
            last = []
            if gain_idx is not None and rstd is None:
                rstd = A.alloc(S, F32)
                sq = [A.alloc(TB, BF16) for _ in range(3)]
                norm_stats(rstd, range(NTB), sq)
            for tt in range(S // 128):
                yt = ytmp[tt % 3]
                ob = ost[tt % 3]
                for c in range(NCH):
                    xa, xk = xs(c, tt * 128, 128)
                    ya = yt.ap[:, c * 128:(c + 1) * 128]
                    yk = yt.k(c * 128, (c + 1) * 128)
                    eng = "dve" if (c % 2 == 0 or gain_idx is not None) else "pool"
                    if gain_idx is not None:
                        ra = rstd.ap[:, tt * 128:(tt + 1) * 128]
                        rk = rstd.k(tt * 128, (tt + 1) * 128)
                        P.op(eng, lambda e, ya=ya, xa=xa, ra=ra, c=c: e.scalar_tensor_tensor(
                            out=ya, in0=xa, scalar=gains_t[:, gain_idx * 8 + c:gain_idx * 8 + c + 1], in1=ra, op0=ALU.mult, op1=ALU.mult),
                            reads=[xk, rk, KC["gains"]], writes=[yk])
                    else:
                        P.op(eng, lambda e, ya=ya, xa=xa: e.tensor_copy(out=ya, in_=xa), reads=[xk], writes=[yk])
                for half in range(2):
                    b = P.ps_alloc()
                    for j in range(4):
                        c = half * 4 + j
                        P.op("pe", lambda e, yt=yt, b=b, j=j, c=c: e.transpose(out=ps_t[:, b, j * 128:(j + 1) * 128], in_=yt.ap[:, c * 128:(c + 1) * 128], identity=ident_t[:, :]),
                             reads=[yt.k(c * 128, (c + 1) * 128), KC["ident"]], writes=[psk(b)])
                    oa = ob.ap[:, half * 512:(half + 1) * 512]
                    ok_ = ob.k(half * 512, (half + 1) * 512)
                    if half == 0:
                        P.op("dve", lambda e, oa=oa, b=b: e.tensor_copy(out=oa, in_=ps_t[:, b, :]), reads=[psk(b)], writes=[ok_])
                    else:
                        P.op("act", lambda e, oa=oa, b=b: e.activation(out=oa, in_=ps_t[:, b, :], func=AF.Copy), reads=[psk(b)], writes=[ok_])
                    P.ps_release(b)
                r0 = seq * S + tt * 128
                last.append(P.dma("pool", lambda e, ob=ob, r0=r0: e.dma_start(out=out_d[r0:r0 + 128, :], in_=ob.ap), "oout%d" % (tt % 3), reads=[ob.k()]))
            A.pop()
            return last[-3:]

        def attention_layer(layer):
            cfg = LAYER_CFG[layer]
            groups = cfg["groups"] if dbg_groups is None else dbg_groups
            G = len(groups)
            win = win_b[layer]
            wo_d = wout_b[layer]
            k_win = SK["win%d" % layer]
            k_wout = SK["wout%d" % layer]
            win3 = win.rearrange("(k p) n -> p k n", p=128)
            A.push()
            KT = [A.alloc(S, BF16) for _ in range(G)]
            VA = [A.alloc(16 * 128, BF16) for _ in range(G)]
            QT = [A.alloc(S, BF16) for _ in range(2)]
            OT = A.alloc(2 * S, BF16)
            accA = A.alloc(S, F32)
            accB = A.alloc(S, F32)
            NPT = 4
            ptw = max(128 + 2 * w_ for (w_, _d) in groups)
            PT = [[A.alloc(ptw, BF16) for _ in range(2)] for _ in range(NPT)]
            NRB = 3 if G == 1 else 2
            ropebufs = [(A.alloc(TB, BF16), A.alloc(TB, F32), A.alloc(TB, F32)) for _ in range(NRB)]
            wk = [A.alloc(NCH * 128, BF16) for _ in range(G)]
            wv = [A.alloc(NCH * 128, BF16) for _ in range(G)]
            wq = [A.alloc(NCH * 128, BF16) for _ in range(2)]
            wo = A.alloc(2 * D, BF16)
            ropecnt = [0]
            qcnt = [0]
            addcnt = [0]

            def rope_stage_a(it):
                wbuf, tb = it["w"], it["tb"]
                b = P.ps_alloc()
                for k in range(NCH):
                    ha, hk = hs(k, tb * TB, TB)
                    P.op("pe", lambda e, b=b, k=k, ha=ha, wbuf=wbuf: e.matmul(ps_t[:, b, :], lhsT=wbuf.ap[:, k * 128:(k + 1) * 128], rhs=ha,
                                                                               start=(k == 0), stop=(k == NCH - 1)),
                         reads=[wbuf.k(), hk], writes=[psk(b)])
                bufs = ropebufs[ropecnt[0] % NRB]
                ropecnt[0] += 1
                qraw = bufs[0]
                P.op("act", lambda e, b=b, qraw=qraw: e.activation(out=qraw.ap, in_=ps_t[:, b, :], func=AF.Copy), reads=[psk(b)], writes=[qraw.k()])
                P.ps_release(b)
                return bufs

            def rope_stage_b(it, bufs):
                qraw, tt_, uu = bufs
                tb, dst, d = it["tb"], it["dst"], it["d"]
                t0 = tb * TB
                b2 = P.ps_alloc()
                P.op("pe", lambda e: e.matmul(ps_t[:, b2, :], lhsT=rperm_t[:, :], rhs=qraw.ap, start=True, stop=True),
                     reads=[qraw.k(), KC["rperm"]], writes=[psk(b2)])
                P.op("dve", lambda e: e.tensor_tensor(out=tt_.ap, in0=ps_t[:, b2, :], in1=ropes_t[:, t0:t0 + TB], op=ALU.mult),
                     reads=[psk(b2), KC["ropes"]], writes=[tt_.k()])
                P.ps_release(b2)
                P.op("pool", lambda e: e.tensor_tensor(out=uu.ap, in0=qraw.ap, in1=ropec_t[:, t0:t0 + TB], op=ALU.mult),
                     reads=[qraw.k(), KC["ropec"]], writes=[uu.k()])
                if d == 1:
                    da = dst.ap[:, t0:t0 + TB]
                    dk = dst.k(t0, t0 + TB)
                    ia, ib = tt_.ap, uu.ap
                else:
                    da = dst.ap.rearrange("p (r u) -> p r u", r=d)[:, :, t0 // d:(t0 + TB) // d]
                    dk = dst.k()
                    ia = tt_.ap.rearrange("p (u r) -> p r u", r=d)
                    ib = uu.ap.rearrange("p (u r) -> p r u", r=d)
                addcnt[0] += 1
                P.op("dve" if addcnt[0] % 2 == 0 else "pool", lambda e: e.tensor_tensor(out=da, in0=ia, in1=ib, op=ALU.add), reads=[tt_.k(), uu.k()], writes=[dk])

            def proj_rope_many(items):
                pend = None
                for it in list(items) + [None]:
                    cur = None
                    if it is not None:
                        cur = (it, rope_stage_a(it))
                    if pend is not None:
                        rope_stage_b(*pend)
                    pend = cur

            def kv_proj(kp):
                items = []
                for g, (w, d) in enumerate(groups):
                    base = g * 1536
                    kcol = base + 1024 + kp * 128
                    vcol = base + 1280 + kp * 128
                    wload(wk[g].ap.rearrange("p (k n) -> p k n", k=NCH), wk[g].k(), win3[:, :, kcol:kcol + 128], k_win)
                    wload(wv[g].ap.rearrange("p (k n) -> p k n", k=NCH), wv[g].k(), win3[:, :, vcol:vcol + 128], k_win)
                    for tb in range(NTB):
                        items.append(dict(w=wk[g], tb=tb, dst=KT[g], d=d))
                proj_rope_many(items)
                for g, (w, d) in enumerate(groups):
                    Ls = S // d
                    for J in range(16):
                        r = (128 * J) // Ls
                        u0 = (128 * J) % Ls
                        if J % 4 == 0:
                            bv = P.ps_alloc()
                        for k in range(NCH):
                            la, lk = hs(k, r + d * u0, 128, step=d)
                            P.op("pe", lambda e, bv=bv, k=k, la=la, J=J, g=g: e.matmul(ps_t[:, bv, (J % 4) * 128:(J % 4) * 128 + 128], lhsT=la,
                                                                                       rhs=wv[g].ap[:, k * 128:(k + 1) * 128], start=(k == 0), stop=(k == NCH - 1)),
                                 reads=[lk, wv[g].k()], writes=[psk(bv)])
                        if J % 4 == 3:
                            J0 = J - 3
                            va3 = VA[g].ap[:, J0 * 128:(J0 + 4) * 128]
                            P.op("act", lambda e, va3=va3, bv=bv: e.activation(out=va3, in_=ps_t[:, bv, :], func=AF.Copy), reads=[psk(bv)], writes=[VA[g].k(J0 * 128, (J0 + 4) * 128)])
                            P.ps_release(bv)

            def q_proj(unit):
                kp, j, g = unit
                w, d = groups[g]
                hA = 8 * kp + j
                hB = 8 * kp + 4 + j
                qb = QT[qcnt[0] % 2]
                wqb = wq[qcnt[0] % 2]
                qcnt[0] += 1
                wq3 = wqb.ap.rearrange("p (k n) -> p k n", k=NCH)
                wload(wq3[:, :, 0:64], wqb.k(), win3[:, :, g * 1536 + hA * 64:g * 1536 + hA * 64 + 64], k_win)
                wload(wq3[:, :, 64:128], wqb.k(), win3[:, :, g * 1536 + hB * 64:g * 1536 + hB * 64 + 64], k_win)
                proj_rope_many([dict(w=wqb, tb=tb, dst=qb, d=d) for tb in range(NTB)])
                return qb

            def attention_unit(unit, qb):
                kp, j, g = unit
                w, d = groups[g]
                Ls = S // d
                mask_t = m128_t if w == 128 else m64_t
                tiles_per_sub = Ls // 128
                kt = KT[g]
                va = VA[g]
                rng = {}
                for J in range(16):
                    jl = J % tiles_per_sub
                    sub0 = (J // tiles_per_sub) * Ls
                    qlo = max(0, 128 * jl - w)
                    qhi = min(Ls, 128 * jl + 128 + w)
                    rng[J] = (sub0 + qlo, sub0 + qhi, qlo - (128 * jl - w))
                last_J = {}
                for J in range(16):
                    for m in range(rng[J][0] // 512, (rng[J][1] - 1) // 512 + 1):
                        last_J[m] = J
                acc_banks = {}
                started = set()

                def st_tile(J):
                    glo, ghi, moff = rng[J]
                    nq = ghi - glo
                    pts = PT[J % NPT]
                    banks = [P.ps_alloc(), P.ps_alloc()]
                    for hh in range(2):
                        b = banks[hh]
                        P.op("pe", lambda e, b=b, nq=nq, moff=moff: e.matmul(ps_t[:, b, 0:nq], lhsT=identb_t[:, :], rhs=mask_t[:, moff:moff + nq],
                                                                            start=True, stop=False, skip_group_check=True),
                             reads=[KC["identb"], KC["m128"], KC["m64"]], writes=[psk(b)])
                    for hh in range(2):
                        rows = slice(hh * 64, hh * 64 + 64)
                        b = banks[hh]
                        P.op("pe", lambda e, b=b, rows=rows, J=J, nq=nq, glo=glo: e.matmul(
                            ps_t[:, b, 0:nq], lhsT=kt.ap[rows, 128 * J:128 * J + 128], rhs=qb.ap[rows, glo:glo + nq],
                            start=False, stop=True, skip_group_check=True),
                            reads=[kt.k(128 * J, 128 * J + 128), qb.k(glo, glo + nq)], writes=[psk(b)])
                    for hh in range(2):
                        b = banks[hh]
                        pt = pts[hh]
                        P.op("act", lambda e, b=b, pt=pt, nq=nq: e.activation(out=pt.ap[:, 0:nq], in_=ps_t[:, b, 0:nq], func=AF.Exp, scale=0.125),
                             reads=[psk(b)], writes=[pt.k(0, nq)])
                        P.ps_release(b)

                def pv_tile(J):
                    glo, ghi, _ = rng[J]
                    for m in range(glo // 512, (ghi - 1) // 512 + 1):
                        lo = max(glo, 512 * m)
                        hi = min(ghi, 512 * (m + 1))
                        if m not in acc_banks:
                            acc_banks[m] = (P.ps_alloc(), P.ps_alloc())
                        for (bi, use_v) in ((0, True), (1, False)):
                            bO = acc_banks[m][bi]
                            for hh in range(2):
                                first = (m, bi, hh) not in started
                                started.add((m, bi, hh))
                                pt = PT[J % NPT][hh]
                                lhs = va.ap[:, J * 128 + hh * 64:J * 128 + hh * 64 + 64] if use_v else ones_t[:, 0:64]
                                lk = va.k(J * 128, (J + 1) * 128) if use_v else KC["ones"]
                                P.op("pe", lambda e, bO=bO, pt=pt, lo=lo, hi=hi, glo=glo, m=m, first=first, lhs=lhs, hh=hh: e.matmul(
                                    ps_t[hh * 64:hh * 64 + 64, bO, lo - 512 * m:hi - 512 * m], lhsT=lhs, rhs=pt.ap[:, lo - glo:hi - glo],
                                    start=first, stop=False, skip_group_check=True),
                                    reads=[lk, pt.k(lo - glo, hi - glo)], writes=[psk(bO)])
                        if last_J[m] == J:
                            bA, bB = acc_banks.pop(m)
                            p0 = 512 * m
                            for acc, bO in ((accA, bA), (accB, bB)):
                                if d == 1:
                                    da = acc.ap[:, p0:p0 + 512]
                                    src = ps_t[:, bO, :]
                                elif d == 4:
                                    r = p0 // Ls
                                    da = acc.ap[:, r:S:4]
                                    src = ps_t[:, bO, :]
                                else:
                                    r0 = p0 // Ls
                                    da = acc.ap.rearrange("p (u r) -> p r u", r=16)[:, r0:r0 + 4, :]
                                    src = ps_t[:, bO, :].rearrange("p (r u) -> p r u", r=4)
                                dk = acc.k(p0, p0 + 512) if d == 1 else acc.k()
                                if g == 0:
                                    P.op("act", lambda e, da=da, src=src: e.activation(out=da, in_=src, func=AF.Copy), reads=[psk(bO)], writes=[dk])
                                else:
                                    P.op("dve", lambda e, da=da, src=src: e.tensor_tensor(out=da, in0=da, in1=src, op=ALU.add), reads=[psk(bO), dk], writes=[dk])
                                P.ps_release(bO)

                LAG = 3
                for step in range(16 + LAG):
                    if step < 16:
                        st_tile(step)
                    J = step - LAG
                    if J >= 0:
                        pv_tile(J)

            def normalize(kp, j):
                c = kp * 4 + j
                pl = j % 2
                if cfg["sink"]:
                    P.op("dve", lambda e, c=c: e.tensor_scalar(out=accB.ap, in0=accB.ap, scalar1=es_t[:, c:c + 1], scalar2=None, op0=ALU.add),
                         reads=[accB.k(), KC["es"]], writes=[accB.k()])
                for tb in range(NTB):
                    t0 = tb * TB
                    P.op("dve", lambda e, t0=t0: e.reciprocal(out=accB.ap[:, t0:t0 + TB], in_=accB.ap[:, t0:t0 + TB]), reads=[accB.k(t0, t0 + TB)], writes=[accB.k(t0, t0 + TB)])
                    oa = OT.ap[:, pl * S + t0:pl * S + t0 + TB]
                    P.op("pool", lambda e, oa=oa, t0=t0: e.tensor_tensor(out=oa, in0=accA.ap[:, t0:t0 + TB], in1=accB.ap[:, t0:t0 + TB], op=ALU.mult),
                         reads=[accA.k(t0, t0 + TB), accB.k(t0, t0 + TB)], writes=[OT.k(pl * S + t0, pl * S + t0 + TB)])

            def out_proj(kp, jj):
                wo3 = wo.ap.rearrange("p (j n) -> p j n", j=2)
                for s_ in range(2):
                    j = 2 * jj + s_
                    hA = 8 * kp + j
                    hB = 8 * kp + 4 + j
                    wload(wo3[0:64, s_, :], wo.k(), wo_d[hA * 64:hA * 64 + 64, :], k_wout)
                    wload(wo3[64:128, s_, :], wo.k(), wo_d[hB * 64:hB * 64 + 64, :], k_wout)
                for dc in range(NCH):
                    for tb in range(NTB):
                        b = P.ps_alloc()
                        for j in range(2):
                            P.op("pe", lambda e, b=b, j=j, dc=dc, tb=tb: e.matmul(ps_t[:, b, :], lhsT=wo.ap[:, j * D + dc * 128:j * D + dc * 128 + 128],
                                                                                   rhs=OT.ap[:, j * S + tb * TB:j * S + (tb + 1) * TB], start=(j == 0), stop=(j == 1)),
                                 reads=[wo.k(), OT.k(j * S + tb * TB, j * S + (tb + 1) * TB)], writes=[psk(b)])
                        xa, xk = xs(dc, tb * TB, TB)
                        P.op("dve", lambda e, xa=xa, b=b: e.tensor_tensor(out=xa, in0=xa, in1=ps_t[:, b, :], op=ALU.add), reads=[xk, psk(b)], writes=[xk])
                        P.ps_release(b)

            kv_proj(0)
            for kp in range(2):
                if kp == 1 and layer == 0:
                    do_casts("attn0")
                units = [(kp, j, g) for j in range(4) for g in range(G)]
                qb_next = q_proj(units[0])
                for i, unit in enumerate(units):
                    qb_cur = qb_next
                    if i + 1 < len(units):
                        qb_next = q_proj(units[i + 1])
                    attention_unit(unit, qb_cur)
                    if unit[2] == G - 1:
                        j = unit[1]
                        normalize(kp, j)
                        if j % 2 == 1:
                            if j == 3 and kp + 1 < 2:
                                kv_proj(kp + 1)
                            out_proj(kp, j // 2)
            A.pop()

        def ffn_layer(layer, after_half=None):
            A.push()
            TH = 1024
            actT = A.alloc(NFC * TH, BF16)
            wgb = [A.alloc(NCH * 256, BF16) for _ in range(2)]
            wub = [A.alloc(NCH * 256, BF16) for _ in range(2)]
            wdb = [A.alloc(NFC * 128, BF16) for _ in range(2)]
            sg = [A.alloc(TB, F32) for _ in range(2)]
            cnt = 0
            if layer == 0:
                do_casts("ffn0")
            for half in range(S // TH):
                for fp in range(NFC // 2):
                    gb, ub = wgb[fp % 2], wub[fp % 2]
                    wload(gb.ap.rearrange("p (k n) -> p k n", k=NCH), gb.k(), wg_b[layer].rearrange("(k p) n -> p k n", p=128)[:, :, fp * 256:(fp + 1) * 256], SK["wg%d" % layer])
                    wload(ub.ap.rearrange("p (k n) -> p k n", k=NCH), ub.k(), wu_b[layer].rearrange("(k p) n -> p k n", p=128)[:, :, fp * 256:(fp + 1) * 256], SK["wu%d" % layer])
                    for fc in range(2):
                        f = fp * 2 + fc
                        for tbl in range(TH // TB):
                            tb = half * (TH // TB) + tbl
                            bg = P.ps_alloc()
                            bu = P.ps_alloc()
                            for (wb, b) in ((gb, bg), (ub, bu)):
                                for k in range(NCH):
                                    ha, hk = hs(k, tb * TB, TB)
                                    P.op("pe", lambda e, wb=wb, b=b, k=k, ha=ha, fc=fc: e.matmul(ps_t[:, b, :], lhsT=wb.ap[:, k * 256 + fc * 128:k * 256 + fc * 128 + 128], rhs=ha,
                                                                                                 start=(k == 0), stop=(k == NCH - 1)),
                                         reads=[wb.k(), hk], writes=[psk(b)])
                            s_ = sg[cnt % 2]
                            cnt += 1
                            P.op("act", lambda e, s_=s_, bg=bg: e.activation(out=s_.ap, in_=ps_t[:, bg, :], func=AF.Silu), reads=[psk(bg)], writes=[s_.k()])
                            P.ps_release(bg)
                            aa = actT.ap[:, f * TH + tbl * TB:f * TH + (tbl + 1) * TB]
                            ak = actT.k(f * TH + tbl * TB, f * TH + (tbl + 1) * TB)
                            P.op("dve", lambda e, aa=aa, s_=s_, bu=bu: e.tensor_tensor(out=aa, in0=s_.ap, in1=ps_t[:, bu, :], op=ALU.mult), reads=[s_.k(), psk(bu)], writes=[ak])
                            P.ps_release(bu)
                for dc in range(NCH):
                    db = wdb[dc % 2]
                    db3 = db.ap.rearrange("p (k n) -> p k n", k=NFC)
                    wd3 = wd_b[layer].rearrange("(k p) n -> p k n", p=128)[:, :, dc * 128:(dc + 1) * 128]
                    wload(db3[:, 0:11, :], db.k(0, 11 * 128), wd3[:, 0:11, :], SK["wd%d" % layer])
                    wload(db3[:, 11:22, :], db.k(11 * 128, 22 * 128), wd3[:, 11:22, :], SK["wd%d" % layer])
                    for tbl in range(TH // TB):
                        tb = half * (TH // TB) + tbl
                        b = P.ps_alloc()
                        for f in range(NFC):
                            P.op("pe", lambda e, b=b, f=f, db=db, tbl=tbl: e.matmul(ps_t[:, b, :], lhsT=db.ap[:, f * 128:(f + 1) * 128],
                                                                                     rhs=actT.ap[:, f * TH + tbl * TB:f * TH + (tbl + 1) * TB], start=(f == 0), stop=(f == NFC - 1)),
                                 reads=[db.k(), actT.k(f * TH + tbl * TB, f * TH + (tbl + 1) * TB)], writes=[psk(b)])
                        xa, xk = xs(dc, tb * TB, TB)
                        P.op("dve", lambda e, xa=xa, b=b: e.tensor_tensor(out=xa, in0=xa, in1=ps_t[:, b, :], op=ALU.add), reads=[xk, psk(b)], writes=[xk])
                        P.ps_release(b)
                if after_half is not None:
                    after_half(half)
            A.pop()

        finals = []
        layers = list(layer_list if layer_list is not None else range(nlayers))
        for seq in range(nseq):
            load_x(seq)
            do_casts("after_load")
            A.push()
            rstd_f = None
            pre_normed = False
            for li, layer in enumerate(layers):
                if do_attn:
                    if not pre_normed:
                        rmsnorm_to_hT(layer)
                    attention_layer(layer)
                pre_normed = False
                if do_ffn:
                    rmsnorm_to_hT(2 + layer)
                    cb = None
                    if li + 1 < len(layers) and do_attn:
                        nxt = layers[li + 1]
                        cb = lambda half, nxt=nxt: rmsnorm_to_hT(nxt, [2 * half, 2 * half + 1])
                        pre_normed = True
                    elif li + 1 == len(layers) and final_norm:
                        rstd_f = A.alloc(S, F32)
                        sq_f = [A.alloc(TB, BF16) for _ in range(2)]
                        cb = lambda half: norm_stats(rstd_f, [2 * half, 2 * half + 1], sq_f)
                    ffn_layer(layer, cb)
            finals += store_out(seq, 4 if final_norm else None, rstd_f)
            A.pop()
        counts = P.build(final_wait_ops=finals)
        counts['arena_hw'] = getattr(A, 'hw', 0)
        counts['arena'] = ARENA_BYTES
    return nc, counts, len(P.ops)


def _consts():
    inv_freq = (1.0 / (np.float32(10000.0) ** (np.arange(0, 64, 2, dtype=np.float32) / np.float32(64)))).astype(np.float32)
    ang = (np.arange(S, dtype=np.float32)[:, None] * inv_freq[None, :]).astype(np.float32)
    cos = np.cos(ang).astype(np.float32).T
    sin = np.sin(ang).astype(np.float32).T
    ropec = np.tile(cos, (4, 1)).astype(np.float32)
    sgn = np.where((np.arange(128) % 64) < 32, -1.0, 1.0).astype(np.float32)[:, None]
    ropes = (np.tile(sin, (4, 1)) * sgn).astype(np.float32)
    ident = np.eye(128, dtype=np.float32)
    rperm = np.zeros((128, 128), np.float32)
    for m in range(128):
        partner = m + 32 if (m % 64) < 32 else m - 32
        rperm[partner, m] = 1.0
    b = np.arange(128)[:, None]
    a128 = np.arange(384)[None, :]
    m128 = (((b <= a128) & (a128 <= b + 256)).astype(np.float32) - 1.0) * 30000.0
    a64 = np.arange(256)[None, :]
    m64 = (((b <= a64) & (a64 <= b + 128)).astype(np.float32) - 1.0) * 30000.0
    return dict(ropec=ropec, ropes=ropes, ident=ident, rperm=rperm, m128=m128, m64=m64)


_CACHE = {}


def _prep_shared(a_w_in, a_sink, a_w_out, b_w_in, b_w_out, norm_mix, norm_ffn, w_gate, w_up, w_down, final_norm):
    f = lambda v: np.ascontiguousarray(np.asarray(v, dtype=np.float32))
    gains_src = [f(norm_mix)[0], f(norm_mix)[1], f(norm_ffn)[0], f(norm_ffn)[1], f(final_norm)]
    gains = np.concatenate([g.reshape(8, 128).T for g in gains_src], axis=1)
    sk = f(a_sink)[0]
    colsA = [8 * kp + j for kp in range(2) for j in range(4)]
    colsB = [8 * kp + 4 + j for kp in range(2) for j in range(4)]
    sinkb = np.concatenate([np.broadcast_to(sk[colsA][None, :], (64, 8)), np.broadcast_to(sk[colsB][None, :], (64, 8))], axis=0)
    d = dict(a_w_in=f(a_w_in)[0], a_w_out=f(a_w_out)[0], b_w_in=f(b_w_in)[0], b_w_out=f(b_w_out)[0],
             w_gate=f(w_gate), w_up=f(w_up), w_down=f(w_down),
             gains=np.ascontiguousarray(gains), sinkb=np.ascontiguousarray(sinkb))
    d.update(_consts())
    return d


def kernel(x, a_w_in, a_sink, a_w_out, b_w_in, b_w_out, norm_mix, norm_ffn, w_gate, w_up, w_down, final_norm):
    x = np.asarray(x, dtype=np.float32)
    B = x.shape[0]
    per = B // NCORES
    if "nc" not in _CACHE:
        _CACHE["nc"] = build_program(per, 2, True)[0]
    nc = _CACHE["nc"]
    shared = _prep_shared(a_w_in, a_sink, a_w_out, b_w_in, b_w_out, norm_mix, norm_ffn, w_gate, w_up, w_down, final_norm)
    in_maps = []
    for i in range(NCORES):
        m = dict(shared)
        m["x"] = np.ascontiguousarray(x[i * per:(i + 1) * per].reshape(per * S, D))
        in_maps.append(m)
    res = run_bass_kernel_spmd(nc, in_maps, core_ids=list(range(NCORES)))
    out = np.concatenate([np.asarray(r["out"], dtype=np.float32).reshape(per, S, D) for r in res.results], axis=0)
    return out
```
